# Optimizing a Trainium2 kernel written in Bass

```python
import jax, jax.numpy as jnp
from jax import lax
import numpy as np

D_MODEL = 2048
BATCH = 16
SEQ = 2048
DEPTH = 2

HEAD_DIM = 64
D_MIX = D_MODEL
GROUP_WIDTH = D_MIX // 4
NORM_EPS = 1e-6
D_FF = 5632
FFN_RES_WEIGHT = 0.5

RWKV_HEADS = GROUP_WIDTH // HEAD_DIM
RWKV_W_RANK = 64
RWKV_A_RANK = 64
RWKV_G_RANK = 128
RWKV_LN_EPS = 64e-5

ATTN_HEADS = GROUP_WIDTH // HEAD_DIM
ATTN_KV_HEADS = 2
ATTN_WINDOW = 128
ATTN_BLOCK = 128
ROPE_THETA = 500000.0
ROPE_DIM = HEAD_DIM // 4

S5_CHANNELS = GROUP_WIDTH
S5_GROUP = 16
S5_GROUPS = S5_CHANNELS // S5_GROUP
S5_STATE = 64
S5_DT_MIN = 0.001
S5_DT_MAX = 0.1

HGRN_HEADS = GROUP_WIDTH // HEAD_DIM
HGRN_DK = 64
HGRN_DV = GROUP_WIDTH // HGRN_HEADS
HGRN_CHUNK = 16

RWKV_IN = 3 * GROUP_WIDTH + RWKV_W_RANK + RWKV_A_RANK + RWKV_G_RANK
ATTN_IN = (ATTN_HEADS + 2 * ATTN_KV_HEADS) * HEAD_DIM
S5_IN = S5_CHANNELS
HGRN_IN = 2 * HGRN_HEADS * HGRN_DK + 2 * HGRN_HEADS * HGRN_DV
D_IN = RWKV_IN + ATTN_IN + S5_IN + HGRN_IN

kernel_name = 'hybrid_parallel_head_groups_trunk'


def rmsnorm(x, g, eps=NORM_EPS):
    xf = x.astype(jnp.float32)
    y = xf * lax.rsqrt(jnp.mean(xf * xf, axis=-1, keepdims=True) + eps)
    return (y * g.astype(jnp.float32)).astype(x.dtype)


def swiglu_ffn(x, norm_g, w_gate, w_up, w_down):
    h = rmsnorm(x, norm_g)
    return (jax.nn.silu(h @ w_gate) * (h @ w_up)) @ w_down


def token_shift(p):
    return jnp.pad(p, ((0, 0), (1, 0), (0, 0)))[:, :-1]


def rwkv7_time_mix(p, mu, w0, w_up, a0, a_up, g_up, k_k, k_a, r_k, ln_w, ln_b):
    B_, S_, _ = p.shape
    H, N = RWKV_HEADS, HEAD_DIM
    f32 = jnp.float32
    p = p + (token_shift(p) - p) * mu
    cut = [GROUP_WIDTH, 2 * GROUP_WIDTH, 3 * GROUP_WIDTH,
           3 * GROUP_WIDTH + RWKV_W_RANK, 3 * GROUP_WIDTH + RWKV_W_RANK + RWKV_A_RANK]
    r, k, v, w_lo, a_lo, g_lo = jnp.split(p, cut, axis=-1)
    w_log = -jax.nn.softplus(-(w0 + jnp.tanh(w_lo) @ w_up)) - 0.5
    a = jax.nn.sigmoid(a0 + a_lo @ a_up)
    g = jax.nn.sigmoid(g_lo) @ g_up
    heads = lambda t: t.astype(f32).reshape(B_, S_, H, N)
    decay = jnp.exp(-jnp.exp(heads(w_log)))
    a = heads(a)
    kk = heads(k * k_k)
    kk = kk * lax.rsqrt(jnp.maximum(jnp.sum(kk * kk, axis=-1, keepdims=True), 1e-24))
    r, v = heads(r), heads(v)
    k = heads(k) * (1.0 + (a - 1.0) * k_a.astype(f32).reshape(H, N))
    xs = tuple(jnp.moveaxis(t, 1, 0) for t in (r, decay, k, v, kk, kk * a))

    def step(state, inp):
        r_t, w_t, k_t, v_t, kk_t, b_t = inp
        sa = jnp.einsum('bhij,bhj->bhi', state, -kk_t)
        state = (state * w_t[:, :, None, :] + sa[..., None] * b_t[:, :, None, :]
                 + v_t[..., None] * k_t[:, :, None, :])
        return state, jnp.einsum('bhij,bhj->bhi', state, r_t)

    _, ys = lax.scan(step, jnp.zeros((B_, H, N, N), f32), xs)
    o = jnp.moveaxis(ys, 0, 1)
    mean = jnp.mean(o, axis=-1, keepdims=True)
    var = jnp.mean(jnp.square(o - mean), axis=-1, keepdims=True)
    o = ((o - mean) * lax.rsqrt(var + RWKV_LN_EPS)).reshape(B_, S_, H * N) * ln_w + ln_b
    bonus = jnp.sum(r * k * r_k.astype(f32), axis=-1, keepdims=True) * v
    return ((o + bonus.reshape(B_, S_, H * N)) * g).astype(p.dtype)


def apply_partial_rope(x, cos, sin):
    half = ROPE_DIM // 2
    x1, x2, rest = x[..., :half], x[..., half:ROPE_DIM], x[..., ROPE_DIM:]
    return jnp.concatenate([x1 * cos - x2 * sin, x2 * cos + x1 * sin, rest], axis=-1)


def swa_sink_attention(p, positions, q_norm, k_norm, sinks):
    B_, S_, _ = p.shape
    H, KV, D = ATTN_HEADS, ATTN_KV_HEADS, HEAD_DIM
    G = H // KV
    f32 = jnp.float32
    q, k, v = jnp.split(p, [H * D, (H + KV) * D], axis=-1)
    q = rmsnorm(q.reshape(B_, S_, H, D), q_norm).astype(f32)
    k = rmsnorm(k.reshape(B_, S_, KV, D), k_norm).astype(f32)
    v = v.reshape(B_, S_, KV, D).astype(f32)
    inv_freq = ROPE_THETA ** (-jnp.arange(0, ROPE_DIM, 2, dtype=f32) / ROPE_DIM)
    ang = positions.astype(f32)[..., None] * inv_freq
    cos, sin = jnp.cos(ang)[:, :, None, :], jnp.sin(ang)[:, :, None, :]
    q = apply_partial_rope(q, cos, sin)
    k = apply_partial_rope(k, cos, sin)
    nb = S_ // ATTN_BLOCK
    qb = q.reshape(B_, nb, ATTN_BLOCK, KV, G, D) * (D ** -0.5)
    kb = k.reshape(B_, nb, ATTN_BLOCK, KV, D)
    vb = v.reshape(B_, nb, ATTN_BLOCK, KV, D)
    shift = lambda t: jnp.pad(t, ((0, 0), (1, 0), (0, 0), (0, 0), (0, 0)))[:, :-1]
    kcat = jnp.concatenate([shift(kb), kb], axis=2)
    vcat = jnp.concatenate([shift(vb), vb], axis=2)
    s = jnp.einsum('bnqhgd,bnkhd->bnhgqk', qb, kcat)
    blk = jnp.arange(nb)[:, None] * ATTN_BLOCK
    qpos = blk + jnp.arange(ATTN_BLOCK)[None, :]
    kpos = blk - ATTN_BLOCK + jnp.arange(2 * ATTN_BLOCK)[None, :]
    diff = qpos[:, :, None] - kpos[:, None, :]
    mask = (diff >= 0) & (diff < ATTN_WINDOW) & (kpos[:, None, :] >= 0)
    s = jnp.where(mask[None, :, None, None], s, -jnp.inf)
    sink = jnp.broadcast_to(sinks.astype(f32).reshape(1, 1, KV, G, 1, 1), s.shape[:-1] + (1,))
    probs = jax.nn.softmax(jnp.concatenate([s, sink], axis=-1), axis=-1)[..., :-1]
    o = jnp.einsum('bnhgqk,bnkhd->bnqhgd', probs, vcat)
    return o.reshape(B_, S_, H * D).astype(p.dtype)


def _linear_combine(e1, e2):
    a1, b1 = e1
    a2, b2 = e2
    return a1 * a2, a2 * b1 + b2


def s5_mix(u, lam_re, lam_im, log_step, b_re, b_im, c_re, c_im, d_skip, glu_w, glu_b):
    B_, S_, _ = u.shape
    f32 = jnp.float32
    uf = u.astype(f32)
    lam = lax.complex(lam_re.astype(f32), lam_im.astype(f32))
    dt = jnp.exp(log_step.astype(f32))[:, None]
    lam_bar = jnp.exp(lam * dt)
    b_bar = ((lam_bar - 1.0) / lam)[..., None] * lax.complex(b_re.astype(f32), b_im.astype(f32))
    c = lax.complex(c_re.astype(f32), c_im.astype(f32))
    ug = uf.reshape(B_, S_, S5_GROUPS, S5_GROUP)
    bu = jnp.einsum('gph,bsgh->bsgp', b_bar, ug.astype(jnp.complex64))
    a = jnp.broadcast_to(lam_bar, (1, S_, S5_GROUPS, S5_STATE))
    _, states = lax.associative_scan(_linear_combine, (a, bu), axis=1)
    y = jnp.einsum('ghp,bsgp->bsgh', c, states).real.reshape(B_, S_, S5_CHANNELS)
    y = y + d_skip.astype(f32) * uf
    z = jax.nn.gelu(y, approximate=False)
    out = z * jax.nn.sigmoid(z @ glu_w.astype(f32) + glu_b.astype(f32))
    return out.astype(u.dtype)


def hgrn2_mix(p, lower_bound, g_norm):
    B_, S_, _ = p.shape
    H, DK, DV, C = HGRN_HEADS, HGRN_DK, HGRN_DV, HGRN_CHUNK
    f32 = jnp.float32
    nc = S_ // C
    q, f, i, g = jnp.split(p, [H * DK, 2 * H * DK, 2 * H * DK + H * DV], axis=-1)
    q = jax.nn.silu(q.astype(f32)).reshape(B_, S_, H, DK) * (DK ** -0.5)
    lb = lower_bound.astype(f32).reshape(H, DK)
    f_gate = lb + (1.0 - lb) * jax.nn.sigmoid(f.astype(f32).reshape(B_, S_, H, DK))
    log_f = jnp.log(f_gate)
    k = 1.0 - f_gate
    v = i.astype(f32).reshape(B_, S_, H, DV)
    to_chunks = lambda t: t.reshape(B_, nc, C, H, t.shape[-1]).transpose(0, 3, 1, 2, 4)
    qc, kc, vc, lfc = to_chunks(q), to_chunks(k), to_chunks(v), to_chunks(log_f)
    b = jnp.cumsum(lfc, axis=3)
    bm = b[:, :, :, C // 2 - 1:C // 2]
    att = jnp.einsum('bhnid,bhnjd->bhnij', qc * jnp.exp(b - bm), kc * jnp.exp(bm - b))
    tri = jnp.tril(jnp.ones((C, C), dtype=bool))
    o_intra = jnp.einsum('bhnij,bhnjv->bhniv', jnp.where(tri, att, 0.0), vc)
    b_last = b[:, :, :, -1]
    kv = jnp.einsum('bhnjd,bhnjv->bhndv', kc * jnp.exp(b_last[:, :, :, None] - b), vc)

    def step(state, inp):
        dec, kv_n = inp
        return state * dec[..., None] + kv_n, state

    _, s_prev = lax.scan(step, jnp.zeros((B_, H, DK, DV), f32),
                         (jnp.moveaxis(jnp.exp(b_last), 2, 0), jnp.moveaxis(kv, 2, 0)))
    o_inter = jnp.einsum('bhnid,nbhdv->bhniv', qc * jnp.exp(b), s_prev)
    o = (o_intra + o_inter).transpose(0, 2, 3, 1, 4).reshape(B_, S_, H, DV)
    o = rmsnorm(o, g_norm) * jax.nn.silu(g.astype(f32).reshape(B_, S_, H, DV))
    return o.reshape(B_, S_, H * DV).astype(p.dtype)


def setup_inputs(seed: int = 0) -> dict:
    key = jax.random.key(seed)
    ks = iter(jax.random.split(key, 40))
    f32 = jnp.float32
    L, D, F, GW = DEPTH, D_MODEL, D_FF, GROUP_WIDTH
    nrm = lambda shape, scale: scale * jax.random.normal(next(ks), shape, f32)
    gain = lambda shape: 1.0 + 0.05 * jax.random.normal(next(ks), shape, f32)
    uni = lambda shape, lo, hi: jax.random.uniform(next(ks), shape, f32, lo, hi)
    x = jax.random.normal(next(ks), (BATCH, SEQ, D), f32)
    positions = (jnp.arange(SEQ, dtype=jnp.int32)[None, :]
                 + jax.random.randint(next(ks), (BATCH, 1), 0, 4096, dtype=jnp.int32))
    return {
        'x': x,
        'positions': positions,
        'ffn1_norm': gain((L, D)),
        'ffn1_w_gate': nrm((L, D, F), D ** -0.5),
        'ffn1_w_up': nrm((L, D, F), D ** -0.5),
        'ffn1_w_down': nrm((L, F, D), F ** -0.5),
        'mix_norm': gain((L, D)),
        'w_in': nrm((L, D, D_IN), D ** -0.5),
        'rwkv_mu': uni((L, RWKV_IN), 0.0, 1.0),
        'rwkv_w0': uni((L, GW), -6.5, -1.5),
        'rwkv_w_up': nrm((L, RWKV_W_RANK, GW), 0.5 * RWKV_W_RANK ** -0.5),
        'rwkv_a0': nrm((L, GW), 0.1),
        'rwkv_a_up': nrm((L, RWKV_A_RANK, GW), RWKV_A_RANK ** -0.5),
        'rwkv_g_up': nrm((L, RWKV_G_RANK, GW), RWKV_G_RANK ** -0.5),
        'rwkv_k_k': 0.85 + nrm((L, GW), 0.05),
        'rwkv_k_a': 1.0 + nrm((L, GW), 0.05),
        'rwkv_r_k': nrm((L, RWKV_HEADS, HEAD_DIM), 0.1),
        'rwkv_ln_w': gain((L, GW)),
        'rwkv_ln_b': nrm((L, GW), 0.02),
        'attn_q_norm': gain((L, HEAD_DIM)),
        'attn_k_norm': gain((L, HEAD_DIM)),
        'attn_sinks': nrm((L, ATTN_HEADS), 0.5),
        's5_lambda_re': -0.5 + nrm((L, S5_GROUPS, S5_STATE), 0.01),
        's5_lambda_im': (jnp.pi * jnp.arange(S5_STATE, dtype=f32))[None, None, :] + nrm((L, S5_GROUPS, S5_STATE), 0.01),
        's5_log_step': uni((L, S5_GROUPS), float(np.log(S5_DT_MIN)), float(np.log(S5_DT_MAX))),
        's5_b_re': nrm((L, S5_GROUPS, S5_STATE, S5_GROUP), (2 * S5_GROUP) ** -0.5),
        's5_b_im': nrm((L, S5_GROUPS, S5_STATE, S5_GROUP), (2 * S5_GROUP) ** -0.5),
        's5_c_re': nrm((L, S5_GROUPS, S5_GROUP, S5_STATE), (2 * S5_STATE) ** -0.5),
        's5_c_im': nrm((L, S5_GROUPS, S5_GROUP, S5_STATE), (2 * S5_STATE) ** -0.5),
        's5_d': nrm((L, S5_CHANNELS), 0.5),
        's5_glu_w': nrm((L, S5_CHANNELS, S5_CHANNELS), S5_CHANNELS ** -0.5),
        's5_glu_b': nrm((L, S5_CHANNELS), 0.02),
        'hgrn_lower_bounds': nrm((L, HGRN_HEADS * HGRN_DK), 0.1),
        'hgrn_g_norm': gain((L, HGRN_DV)),
        'w_out': nrm((L, D_MIX, D), D_MIX ** -0.5),
        'ffn2_norm': gain((L, D)),
        'ffn2_w_gate': nrm((L, D, F), D ** -0.5),
        'ffn2_w_up': nrm((L, D, F), D ** -0.5),
        'ffn2_w_down': nrm((L, F, D), F ** -0.5),
    }


def reference(x, positions, ffn1_norm, ffn1_w_gate, ffn1_w_up, ffn1_w_down, mix_norm, w_in,
              rwkv_mu, rwkv_w0, rwkv_w_up, rwkv_a0, rwkv_a_up, rwkv_g_up, rwkv_k_k, rwkv_k_a,
              rwkv_r_k, rwkv_ln_w, rwkv_ln_b, attn_q_norm, attn_k_norm, attn_sinks,
              s5_lambda_re, s5_lambda_im, s5_log_step, s5_b_re, s5_b_im, s5_c_re, s5_c_im,
              s5_d, s5_glu_w, s5_glu_b, hgrn_lower_bounds, hgrn_g_norm, w_out,
              ffn2_norm, ffn2_w_gate, ffn2_w_up, ffn2_w_down):
    lb_all = jnp.cumsum(jax.nn.softmax(hgrn_lower_bounds.astype(jnp.float32), axis=0), axis=0)
    lb_all = lb_all - lb_all[0]
    cut = [RWKV_IN, RWKV_IN + ATTN_IN, RWKV_IN + ATTN_IN + S5_IN]
    for l in range(DEPTH):
        x = x + FFN_RES_WEIGHT * swiglu_ffn(x, ffn1_norm[l], ffn1_w_gate[l], ffn1_w_up[l], ffn1_w_down[l])
        h = rmsnorm(x, mix_norm[l])
        proj = h @ w_in[l]
        p_a, p_b, p_c, p_d = jnp.split(proj, cut, axis=-1)
        y_a = rwkv7_time_mix(p_a, rwkv_mu[l], rwkv_w0[l], rwkv_w_up[l], rwkv_a0[l], rwkv_a_up[l],
                             rwkv_g_up[l], rwkv_k_k[l], rwkv_k_a[l], rwkv_r_k[l], rwkv_ln_w[l], rwkv_ln_b[l])
        y_b = swa_sink_attention(p_b, positions, attn_q_norm[l], attn_k_norm[l], attn_sinks[l])
        y_c = s5_mix(p_c, s5_lambda_re[l], s5_lambda_im[l], s5_log_step[l], s5_b_re[l], s5_b_im[l],
                     s5_c_re[l], s5_c_im[l], s5_d[l], s5_glu_w[l], s5_glu_b[l])
        y_d = hgrn2_mix(p_d, lb_all[l], hgrn_g_norm[l])
        x = x + jnp.concatenate([y_a, y_b, y_c, y_d], axis=-1) @ w_out[l]
        x = x + FFN_RES_WEIGHT * swiglu_ffn(x, ffn2_norm[l], ffn2_w_gate[l], ffn2_w_up[l], ffn2_w_down[l])
    return x
```

```python
import numpy as np
import concourse.bass as bass
import concourse.mybir as mybir
from concourse.bass_utils import run_bass_kernel_spmd

F32 = mybir.dt.float32
BF16 = mybir.dt.bfloat16
I32 = mybir.dt.int32
AF = mybir.ActivationFunctionType
ALU = mybir.AluOpType
AX = mybir.AxisListType

D = 2048
DFF = 5632
KT = D // 128
FT = DFF // 128
NCORES = 8


class Op:
    __slots__ = ("eng", "fn", "deps", "dma", "semkey", "idx", "eidx", "need", "val", "waits")


class Sched:
    ENGS = ["pe", "act", "dve", "pool", "sp"]

    def __init__(self, nc):
        self.nc = nc
        self.ops = []
        self.eng_ops = {e: [] for e in self.ENGS}
        self.regs = {}
        self.dma_count = {}

    def _overlap(self, e, r):
        return e[0] < r[2] and r[1] < e[1] and e[2] < r[4] and r[3] < e[3]

    def add(self, eng, fn, reads=(), writes=(), dma=False, semkey=None):
        op = Op()
        op.eng, op.fn, op.dma, op.semkey = eng, fn, dma, semkey
        op.idx = len(self.ops)
        deps = set()

        def norm(rg):
            if rg[0] != "ps":
                return rg
            return ("ps", rg[1] // 32 * 32, (rg[2] + 31) // 32 * 32, rg[3] // 512 * 512, (rg[4] + 511) // 512 * 512)
        reads = [norm(r) for r in reads]
        writes = [norm(w) for w in writes]
        for r in reads:
            lst = self.regs.setdefault(r[0], [])
            exact = None
            for e in lst:
                if self._overlap(e, r):
                    if e[4] is not None:
                        deps.add(e[4])
                    if r[0] == "ps":
                        for k_, v_ in e[5].items():
                            if k_ != eng:
                                deps.add(v_)
                    if e[0] == r[1] and e[1] == r[2] and e[2] == r[3] and e[3] == r[4]:
                        exact = e
            if exact is None:
                exact = [r[1], r[2], r[3], r[4], None, {}]
                lst.append(exact)
            exact[5][("d", op.idx) if dma else eng] = op.idx
        for w in writes:
            lst = self.regs.setdefault(w[0], [])
            keep = []
            for e in lst:
                if self._overlap(e, w):
                    if e[4] is not None:
                        deps.add(e[4])
                    deps.update(e[5].values())
                    if w[1] <= e[0] and e[1] <= w[2] and w[3] <= e[2] and e[3] <= w[4]:
                        continue
                keep.append(e)
            keep.append([w[1], w[2], w[3], w[4], op.idx, {}])
            self.regs[w[0]] = keep
        deps.discard(op.idx)
        op.deps = deps
        op.eidx = len(self.eng_ops[eng])
        op.need = False
        op.val = None
        if dma:
            self.dma_count[semkey] = self.dma_count.get(semkey, 0) + 1
            op.val = self.dma_count[semkey]
        self.ops.append(op)
        self.eng_ops[eng].append(op)
        return op

    def emit(self, final_wait_eng="sp"):
        nc = self.nc
        ops = self.ops
        for eng in self.ENGS:
            waited = {}
            dma_issued_view = None
            for op in self.eng_ops[eng]:
                ws = []
                latest = {}
                for d in op.deps:
                    p = ops[d]
                    if p.dma:
                        key = ("dma", p.semkey)
                        cnt = self._dma_issued_before(p.semkey, op.idx)
                        if waited.get(key, 0) < cnt:
                            waited[key] = cnt
                            ws.append((key, cnt))
                    else:
                        if p.eng not in latest or latest[p.eng].eidx < p.eidx:
                            latest[p.eng] = p
                for pe_, p in latest.items():
                    if pe_ == eng:
                        if eng == "pe":
                            continue
                        if op.eidx - p.eidx > 3:
                            continue
                    if waited.get(pe_, -1) >= p.eidx:
                        continue
                    waited[pe_] = p.eidx
                    p.need = True
                    ws.append((pe_, p))
                op.waits = ws
        for eng in self.ENGS:
            c = 0
            for op in self.eng_ops[eng]:
                if not op.dma and op.need:
                    c += 1
                    op.val = c
        import contextlib
        stack = contextlib.ExitStack()
        esem = {e: stack.enter_context(nc.semaphore("s_" + e)) for e in self.ENGS}
        dsem = {}
        for k in self.dma_count:
            dsem[k] = stack.enter_context(nc.semaphore("d_%d" % len(dsem)))
        engobj = {"pe": "tensor", "act": "scalar", "dve": "vector", "pool": "gpsimd", "sp": "sync"}
        with nc.Block() as block:
            for eng in self.ENGS:
                def body(e, eng=eng):
                    for op in self.eng_ops[eng]:
                        for (k, v) in op.waits:
                            if isinstance(k, tuple):
                                e.wait_ge(dsem[k[1]], 16 * v)
                            else:
                                e.wait_ge(esem[k], v.val)
                        ins = op.fn(e)
                        if op.dma:
                            ins.then_inc(dsem[op.semkey], 16)
                        elif op.need:
                            ins.then_inc(esem[eng], 1)
                    if eng == final_wait_eng:
                        for k, c in self.dma_count.items():
                            e.wait_ge(dsem[k], 16 * c)
                getattr(block, engobj[eng])(body)
        stack.close()

    def _dma_issued_before(self, semkey, idx):
        lst = self._dma_lists().get(semkey, [])
        import bisect
        return bisect.bisect_left(lst, idx)

    def _dma_lists(self):
        if getattr(self, "_dl_n", -1) != len(self.ops):
            dl = {}
            for op in self.ops:
                if op.dma:
                    dl.setdefault(op.semkey, []).append(op.idx)
            self._dl = dl
            self._dl_n = len(self.ops)
        return self._dl


def _prod(x):
    r = 1
    for v in x:
        r *= v
    return r


class R:
    __slots__ = ("ap", "region")

    def __init__(self, ap, region):
        self.ap, self.region = ap, region


class Tile:
    def __init__(self, space, base_ap, off, words, dtype=F32):
        self.space, self.off, self.words, self.dtype = space, off, words, dtype
        ap = base_ap[:, off:off + words]
        if dtype != F32:
            ap = ap.bitcast(dtype)
        self.ap = ap
        self.epw = 2 if dtype == BF16 else 1
        self.n = words * self.epw

    def reg(self, lo=None, hi=None, p0=0, p1=128):
        if lo is None:
            return (self.space, p0, p1, self.off, self.off + self.words)
        return (self.space, p0, p1, self.off + lo // self.epw, self.off + (hi + self.epw - 1) // self.epw)

    def r(self, lo=None, hi=None, p0=0, p1=128):
        if lo is None:
            lo, hi = 0, self.n
        return R(self.ap[p0:p1, lo:hi], self.reg(lo, hi, p0, p1))

    def v(self, dims, idx, p0=0, p1=128):
        names = "abcd"[:len(dims)]
        pat = "p (%s) -> p %s" % (" ".join(names), " ".join(names))
        kw = {n_: d for n_, d in zip(names[:-1], dims[:-1])}
        ap = self.ap[p0:p1, 0:_prod(dims)].rearrange(pat, **kw)
        ap = ap[(slice(None),) + tuple(idx)]
        lo = hi = 0
        for i, (d, ix) in enumerate(zip(dims, idx)):
            st = _prod(dims[i + 1:])
            if isinstance(ix, int):
                a0, a1 = ix, ix + 1
            else:
                a0, a1, _ = ix.indices(d)
            lo += a0 * st
            hi += (a1 - 1) * st
        return R(ap, self.reg(lo, hi + 1, p0, p1))


def bc(r, shape, axis):
    return R(r.ap.unsqueeze(axis).broadcast_to(list(shape)), r.region)


class Arena:
    def __init__(self, nc, name, words):
        self.t = nc.alloc_sbuf_tensor(name, [128, words], F32)
        self.ap = self.t.ap()
        self.words = words
        self.ptr = 0
        self.name = name
        self.peak = 0

    def alloc(self, elems, dtype=F32):
        words = elems if dtype != BF16 else (elems + 1) // 2
        words = (words + 7) // 8 * 8
        assert self.ptr + words <= self.words, "arena overflow %s: %d + %d > %d" % (self.name, self.ptr, words, self.words)
        t = Tile("sb", self.ap, self.ptr, words, dtype)
        self.ptr += words
        self.peak = max(self.peak, self.ptr)
        return t

    def mark(self):
        return self.ptr

    def reset(self, m):
        self.ptr = m


PP = np.array([(hh * 4 + m) * 64 + j for m in range(4) for hh in range(2) for j in range(64)])
NCIN = 22 * 256
L_ = 2
TTOK = 512
C_FFN1, C_MIX, C_FFN2 = 0, 16, 32
C_MU, C_OMU = 48, 62
C_W0, C_A0, C_KK, C_KA, C_RK, C_LNW, C_LNB = 76, 80, 84, 88, 92, 96, 100
C_LB0, C_LB1, C_LB, C_OLB = 104, 108, 112, 116
C_GN = 120
C_DSK, C_GLB = 121, 125
C_LRE, C_LIM, C_LST = 129, 145, 161
C_QN, C_KN, C_QNS, C_KNS, C_INVF, C_SGN = 177, 178, 179, 180, 181, 182
C_TH, C_RM, C_KRE, C_NKRE, C_NKIM, C_KIM = 183, 199, 215, 231, 247, 263
NV = 280


def paired(vec512):
    return np.ascontiguousarray(vec512.reshape(2, 4, 64).transpose(0, 2, 1).reshape(128, 4))


def fm(vec, k):
    return np.ascontiguousarray(vec.reshape(k, 128).T)


def pack_layer(inp, l):
    f32 = np.float32
    w_in = inp["w_in"][l]
    A0, B0, C0, D0 = 0, 1792, 2560, 3072
    win = np.zeros((2048, NCIN), f32)
    col = 0

    def put(cols):
        nonlocal col
        n = cols.shape[1]
        win[:, col:col + n] = cols
        col += n
    for blk in range(3):
        put(w_in[:, A0 + blk * 512 + PP])
    put(w_in[:, A0 + 1536:A0 + 1664])
    put(w_in[:, A0 + 1664:A0 + 1792])
    for blk in range(4):
        put(w_in[:, D0 + blk * 512 + PP])
    put(w_in[:, C0:C0 + 512])
    put(w_in[:, B0 + 640:B0 + 768])
    col += 128
    assert col == 36 * 128
    put(w_in[:, B0:B0 + 640])
    sw = []
    for h in range(10):
        c0 = B0 + h * 64
        sw.append(w_in[:, c0 + 8:c0 + 16])
        sw.append(w_in[:, c0:c0 + 8])
    put(np.concatenate(sw, axis=1))
    pv = np.zeros((128, NV), f32)
    pv[:, C_FFN1:C_FFN1 + 16] = fm(inp["ffn1_norm"][l], 16)
    pv[:, C_MIX:C_MIX + 16] = fm(inp["mix_norm"][l], 16)
    pv[:, C_FFN2:C_FFN2 + 16] = fm(inp["ffn2_norm"][l], 16)
    mu = inp["rwkv_mu"][l]
    for blk in range(3):
        pv[:, C_MU + blk * 4:C_MU + blk * 4 + 4] = paired(mu[blk * 512:(blk + 1) * 512])
    pv[:, C_MU + 12] = mu[1536:1664]
    pv[:, C_MU + 13] = mu[1664:1792]
    for c, nm in ((C_W0, "rwkv_w0"), (C_A0, "rwkv_a0"), (C_KK, "rwkv_k_k"), (C_KA, "rwkv_k_a"),
                  (C_LNW, "rwkv_ln_w"), (C_LNB, "rwkv_ln_b")):
        pv[:, c:c + 4] = paired(inp[nm][l])
    pv[:, C_RK:C_RK + 4] = paired(inp["rwkv_r_k"][l].reshape(512))
    pv[:, C_LB0:C_LB0 + 4] = paired(inp["hgrn_lower_bounds"][0])
    pv[:, C_LB1:C_LB1 + 4] = paired(inp["hgrn_lower_bounds"][1])
    pv[:, C_GN] = np.tile(inp["hgrn_g_norm"][l], 2)
    pv[:, C_DSK:C_DSK + 4] = fm(inp["s5_d"][l], 4)
    pv[:, C_GLB:C_GLB + 4] = fm(inp["s5_glu_b"][l], 4)
    pv[:, C_LRE:C_LRE + 16] = fm(inp["s5_lambda_re"][l].reshape(2048), 16)
    pv[:, C_LIM:C_LIM + 16] = fm(inp["s5_lambda_im"][l].reshape(2048), 16)
    pv[:, C_LST:C_LST + 16] = fm(np.repeat(inp["s5_log_step"][l], 64), 16)
    qn, kn = inp["attn_q_norm"][l], inp["attn_k_norm"][l]
    pv[:64, C_QN] = qn
    pv[:64, C_KN] = kn
    pv[:16, C_QNS] = np.concatenate([qn[8:16], qn[0:8]])
    pv[:16, C_KNS] = np.concatenate([kn[8:16], kn[0:8]])
    invf = (500000.0 ** (-np.arange(0, 16, 2, dtype=np.float32) / 16.0)).astype(f32)
    pv[:16, C_INVF] = np.concatenate([invf, invf])
    pv[:16, C_SGN] = np.concatenate([-np.ones(8, f32), np.ones(8, f32)])
    waup = np.concatenate([inp["rwkv_w_up"][l][:, PP], inp["rwkv_a_up"][l][:, PP]], axis=0)
    gup = inp["rwkv_g_up"][l][:, PP]
    bre, bim = inp["s5_b_re"][l], inp["s5_b_im"][l]
    cre, cim = inp["s5_c_re"][l], inp["s5_c_im"][l]
    s5b = np.zeros((128, 2, 16, 128), f32)
    s5c = np.zeros((128, 2, 16, 128), f32)
    for g in range(32):
        st, gl, g8 = g // 2, g % 2, g % 8
        s5b[g8 * 16:(g8 + 1) * 16, 0, st, gl * 64:(gl + 1) * 64] = bre[g].T
        s5b[g8 * 16:(g8 + 1) * 16, 1, st, gl * 64:(gl + 1) * 64] = bim[g].T
        s5c[gl * 64:(gl + 1) * 64, 0, st, g8 * 16:(g8 + 1) * 16] = cre[g].T
        s5c[gl * 64:(gl + 1) * 64, 1, st, g8 * 16:(g8 + 1) * 16] = cim[g].T
    wout = inp["w_out"][l]
    wout_p = np.concatenate([wout[0:512][PP], wout[512:1024], wout[1024:1536], wout[1536:2048][PP]], axis=0)
    sinks_b = np.tile(inp["attn_sinks"][l][None, :], (128, 1))
    return dict(win=win, pv=pv, waup=np.ascontiguousarray(waup), gup=np.ascontiguousarray(gup),
                s5b=s5b.reshape(128, 4096), s5c=s5c.reshape(128, 4096),
                gluw=np.ascontiguousarray(inp["s5_glu_w"][l]), wout=wout_p, sinks=sinks_b.astype(f32))


def pack_all(inp, layers=(0, 1)):
    per = [pack_layer(inp, l) for l in layers]
    out = {k: np.ascontiguousarray(np.stack([p[k] for p in per])) for k in per[0]}
    for nm in ("ffn1_w_gate", "ffn1_w_up", "ffn1_w_down", "ffn2_w_gate", "ffn2_w_up", "ffn2_w_down"):
        out[nm] = np.ascontiguousarray(np.stack([inp[nm][l] for l in layers]))
    return out


PI = float(np.pi)


class Emit:
    def __init__(self, nc, nlayers=2, TT=TTOK, arena_words=49000):
        self.nc = nc
        self.S = Sched(nc)
        self.TT = TT
        self.NL = nlayers
        self.A = Arena(nc, "arena", arena_words)
        A = self.A
        self.ps_t = nc.alloc_psum_tensor("psum", [128, 4096], F32)
        self.ps_ap = self.ps_t.ap()
        self.banks = [Tile("ps", self.ps_ap, i * 512, 512, F32) for i in range(8)]
        self.banks_bf = [Tile("ps", self.ps_ap, i * 512, 512, BF16) for i in range(8)]
        self.bank_rr = 0
        self.xT = A.alloc(KT * TT, F32)
        self.hT = A.alloc(KT * TT, BF16)
        self.ident = A.alloc(128, F32)
        self.ident_bf = A.alloc(128, BF16)
        self.ones_bf = A.alloc(128, BF16)
        self.bones = A.alloc(128, BF16)
        self.msk = A.alloc(3 * 64, F32)
        self.amask = A.alloc(2 * 128, BF16)
        self.cmask = A.alloc(4 * TT, BF16)
        self.iota1 = A.alloc(TT, F32)
        self.cst = A.alloc(8, F32)
        self.pv = A.alloc(nlayers * NV, F32)
        self.sinks = A.alloc(nlayers * 8, F32)
        self.st = []
        for l in range(nlayers):
            d = dict(rwS=A.alloc(512, F32), rwSb=A.alloc(512, BF16), rwC=A.alloc(16, F32),
                     hgS=A.alloc(512, F32), hgSb=A.alloc(512, BF16),
                     kprev=A.alloc(256, BF16), vaug=A.alloc(5 * 2 * 65 + 6, BF16), s5h=A.alloc(32, F32))
            self.st.append(d)
        self.base_mark = A.mark()

    def bank(self):
        i = self.bank_rr % 8
        self.bank_rr += 1
        return self.banks[i], self.banks_bf[i]

    def ck(self, name):
        import os
        ks = os.environ.get('KSTOP', '')
        nm, _, cnt = ks.partition(':')
        if nm == name:
            self.ckn = getattr(self, 'ckn', 0) + 1
            if self.ckn >= int(cnt or 1):
                self.stopped = True

    def op(self, eng, fn, reads, writes, **kw):
        if getattr(self, 'stopped', False) and not kw.get('force'):
            return None
        kw.pop('force', None)
        rg = lambda x: (x.region if isinstance(x, R) else x.reg())
        return self.S.add(eng, fn, [rg(x) for x in reads], [rg(x) for x in writes], **kw)

    def tt(self, o, a, b, op, eng="dve"):
        self.op(eng, lambda e: e.tensor_tensor(o.ap, a.ap, b.ap, op), [a, b], [o])

    def ts(self, o, a, s1, s2, op0, op1=None, eng="dve"):
        rd = [a] + [s for s in (s1, s2) if isinstance(s, R)]
        v1 = s1.ap if isinstance(s1, R) else s1
        v2 = s2.ap if isinstance(s2, R) else s2
        if op1 is None:
            self.op(eng, lambda e: e.tensor_scalar(o.ap, a.ap, v1, None, op0), rd, [o])
        else:
            self.op(eng, lambda e: e.tensor_scalar(o.ap, a.ap, v1, v2, op0, op1), rd, [o])

    def stt(self, o, a, s, b, op0, op1):
        rd = [a, b] + ([s] if isinstance(s, R) else [])
        sv = s.ap if isinstance(s, R) else s
        self.op("dve", lambda e: e.scalar_tensor_tensor(o.ap, a.ap, sv, b.ap, op0, op1), rd, [o])

    def act(self, o, a, func, bias=None, scale=None):
        rd = [a] + ([bias] if isinstance(bias, R) else [])
        kw = {}
        if bias is not None:
            kw["bias"] = bias.ap if isinstance(bias, R) else bias
        if scale is not None:
            if isinstance(scale, R):
                rd.append(scale)
                kw["scale"] = scale.ap
            else:
                kw["scale"] = scale
        self.op("act", lambda e: e.activation(o.ap, a.ap, func, **kw), rd, [o])

    def cp(self, o, a, eng="dve"):
        if eng == "act":
            self.op("act", lambda e: e.copy(o.ap, a.ap), [a], [o])
        else:
            self.op(eng, lambda e: e.tensor_copy(o.ap, a.ap), [a], [o])

    def mm(self, o, l, r, start=True, stop=True):
        self.op("pe", lambda e: e.matmul(o.ap, l.ap, r.ap, start=start, stop=stop), [l, r], [o])

    def tr(self, o, a, ident):
        self.op("pe", lambda e: e.transpose(o.ap, a.ap, ident.ap), [a, ident], [o])

    def dma(self, eng, o, a, semkey, reads=(), writes=(), **kw):
        self.op(eng, lambda e: e.dma_start(out=o, in_=a, **kw), list(reads), list(writes), dma=True, semkey=semkey)

    def memset(self, o, val, eng="pool"):
        self.op(eng, lambda e: e.memset(o.ap, val), [], [o])

    def rsqrt_act(self, o, a, scale, eps_col):
        self.act(o, a, AF.Ln, bias=eps_col, scale=scale)
        self.act(o, o, AF.Exp, scale=-0.5)

    def pvc(self, l, c, n=1, p0=0, p1=128):
        return self.pv.r(l * NV + c, l * NV + c + n, p0, p1)

    def setup_common(self, dr):
        S = self.S
        self.dr = dr
        self.first = True
        ident, ones = self.ident, self.ones_bf
        self.memset(ident.r(), 0.0)
        self.op("pool", lambda e: e.affine_select(ident.ap, ident.ap, [[-1, 128]], ALU.not_equal, 1.0,
                                                  base=0, channel_multiplier=1), [ident.r()], [ident.r()])
        self.cp(self.ident_bf.r(), ident.r(), eng="pool")
        self.memset(ones.r(), 1.0)
        self.memset(self.bones.r(), 1.0)
        self.memset(self.bones.r(64, 128, 0, 64), 0.0)
        self.memset(self.bones.r(0, 64, 64, 128), 0.0)
        msk = self.msk
        self.memset(msk.r(), 1.0)
        for i, (pat, cm, cmp_) in enumerate((([[1, 64]], -1, ALU.is_gt), ([[1, 64]], -1, ALU.is_ge), ([[-1, 64]], 1, ALU.is_gt))):
            v = msk.r(i * 64, (i + 1) * 64, 0, 64)
            self.op("pool", lambda e, v=v, pat=pat, cm=cm, cmp_=cmp_: e.affine_select(
                v.ap, v.ap, pat, cmp_, 0.0, base=0, channel_multiplier=cm), [v], [v])
        am = self.amask
        self.memset(am.r(), 1.0)
        for i, (pat, cm, cmp_) in enumerate((([[1, 128]], -1, ALU.is_ge), ([[-1, 128]], 1, ALU.is_gt))):
            v = am.r(i * 128, (i + 1) * 128)
            self.op("pool", lambda e, v=v, pat=pat, cm=cm, cmp_=cmp_: e.affine_select(
                v.ap, v.ap, pat, cmp_, 0.0, base=0, channel_multiplier=cm), [v], [v])
        self.memset(self.cmask.r(), 1.0)
        self.memset(self.cmask.v((4 * self.TT // 64, 64), (slice(None), 0)), 0.0)
        io = self.iota1
        self.op("pool", lambda e: e.iota(io.ap, [[1, self.TT]], base=1, channel_multiplier=0,
                                         allow_small_or_imprecise_dtypes=True), [], [io.r()])
        for i, val in enumerate((1e-6, 64e-5, 0.0, PI / 2)):
            self.memset(self.cst.r(i, i + 1), val)
        for l in range(self.NL):
            self.dma("sp", self.pv.ap[:, l * NV:(l + 1) * NV], dr["pv"][l], ("pv", l),
                     writes=[self.pv.r(l * NV, (l + 1) * NV)])
            self.dma("sp", self.sinks.ap[:, l * 8:(l + 1) * 8], dr["sinks"][l], ("sk", l),
                     writes=[self.sinks.r(l * 8, (l + 1) * 8)])
        for l in range(self.NL):
            self.layer_consts(l)

    def sincos(self, osin, ocos, ang, tmp, ki):
        PL = 3.1415925
        self.ts(ki, ang, 1.0 / (2.0 * PI), None, ALU.mult)
        self.stt(tmp, ki, -2.0 * PI, ang, ALU.mult, ALU.add)
        self.ts(tmp, tmp, PL, -PL, ALU.min, ALU.max)
        self.act(osin, tmp, AF.Sin)
        self.stt(tmp, tmp, -1.0, tmp, ALU.mult, ALU.max)
        p0, p1 = tmp.region[1], tmp.region[2]
        self.act(ocos, tmp, AF.Sin, bias=self.cst.r(3, 4, p0, p1), scale=-1.0)

    def layer_consts(self, l):
        A = self.A
        m = A.mark()
        pc = lambda c, n=1, p0=0, p1=128: self.pvc(l, c, n, p0, p1)
        self.ts(pc(C_OMU, 14), pc(C_MU, 14), -1.0, 1.0, ALU.mult, ALU.add)
        if l == 0:
            self.memset(pc(C_LB, 4), 0.0)
        else:
            t = A.alloc(4, F32)
            self.tt(t.r(0, 4), pc(C_LB1, 4), pc(C_LB0, 4), ALU.subtract)
            self.act(pc(C_LB, 4), t.r(0, 4), AF.Sigmoid)
        self.ts(pc(C_OLB, 4), pc(C_LB, 4), -1.0, 1.0, ALU.mult, ALU.add)
        sk = self.sinks.r(l * 8, (l + 1) * 8)
        self.act(sk, sk, AF.Exp)
        t = [A.alloc(16, F32) for _ in range(8)]
        dt, sn, cs_, tmp, aa, bb, den, t2 = [x.r() for x in t]
        self.act(dt, pc(C_LST, 16), AF.Exp)
        self.tt(tmp, pc(C_LRE, 16), dt, ALU.mult)
        self.act(pc(C_RM, 16), tmp, AF.Exp)
        self.tt(pc(C_TH, 16), pc(C_LIM, 16), dt, ALU.mult)
        ki16 = A.alloc(16, I32).r()
        self.sincos(sn, cs_, pc(C_TH, 16), tmp, ki16)
        self.tt(aa, pc(C_RM, 16), cs_, ALU.mult)
        self.ts(aa, aa, -1.0, None, ALU.add)
        self.tt(bb, pc(C_RM, 16), sn, ALU.mult)
        self.tt(den, pc(C_LRE, 16), pc(C_LRE, 16), ALU.mult)
        self.tt(tmp, pc(C_LIM, 16), pc(C_LIM, 16), ALU.mult)
        self.tt(den, den, tmp, ALU.add)
        self.op("dve", lambda e: e.reciprocal(den.ap, den.ap), [den], [den])
        self.tt(tmp, aa, pc(C_LRE, 16), ALU.mult)
        self.tt(t2, bb, pc(C_LIM, 16), ALU.mult)
        self.tt(tmp, tmp, t2, ALU.add)
        self.tt(pc(C_KRE, 16), tmp, den, ALU.mult)
        self.tt(tmp, bb, pc(C_LRE, 16), ALU.mult)
        self.tt(t2, aa, pc(C_LIM, 16), ALU.mult)
        self.tt(tmp, tmp, t2, ALU.subtract)
        self.tt(pc(C_KIM, 16), tmp, den, ALU.mult)
        self.ts(pc(C_NKRE, 16), pc(C_KRE, 16), -1.0, None, ALU.mult)
        self.ts(pc(C_NKIM, 16), pc(C_KIM, 16), -1.0, None, ALU.mult)
        A.reset(m)

    def make_scratch(self, L):
        nc = self.nc
        self.sc = {}
        for l in range(L):
            for f in ("ffn1", "ffn2"):
                self.sc[(f + "_g", l)] = nc.dram_tensor("sc_%s_g%d" % (f, l), [22, 128, 4096], BF16).ap()
                self.sc[(f + "_u", l)] = nc.dram_tensor("sc_%s_u%d" % (f, l), [22, 128, 4096], BF16).ap()
                self.sc[(f + "_d", l)] = nc.dram_tensor("sc_%s_d%d" % (f, l), [16, 128, FT * 128], BF16).ap()
            self.sc[("win", l)] = nc.dram_tensor("sc_win%d" % l, [22, 128, 4096], BF16).ap()
            self.sc[("wout", l)] = nc.dram_tensor("sc_wout%d" % l, [16, 128, 2048], BF16).ap()

    def dreg(self, key, i):
        return R(None, ("dr_%s_%d" % key, 0, 1, i, i + 1))

    def load_x_tile(self, x, it):
        TT, A = self.TT, self.A
        m = A.mark()
        nsub = TT // 128
        xin = A.alloc(nsub * D, F32)
        src = x[it * TT:(it + 1) * TT, :].rearrange("(s p) d -> p s d", p=128)
        xin3 = xin.ap.rearrange("p (s d) -> p s d", s=nsub)
        for s in range(nsub):
            self.dma("sp", xin3[:, s, :], src[:, s, :], ("xin", s), writes=[xin.r(s * D, (s + 1) * D)])
        for kt in range(KT):
            b, _ = self.bank()
            for s in range(nsub):
                self.tr(b.r(s * 128, (s + 1) * 128), xin.r(s * D + kt * 128, s * D + (kt + 1) * 128), self.ident)
            self.cp(self.xT.r(kt * TT, (kt + 1) * TT), b.r(0, TT), eng=("act" if kt % 2 == 0 else "dve"))
        A.reset(m)

    def store_x_tile(self, y, it):
        TT, A = self.TT, self.A
        m = A.mark()
        nsub = TT // 128
        xo = A.alloc(nsub * D, F32)
        xo3 = xo.ap.rearrange("p (s d) -> p s d", s=nsub)
        dst = y[it * TT:(it + 1) * TT, :].rearrange("(s p) d -> p s d", p=128)
        for s in range(nsub):
            for kq in range(KT // 4):
                b, _ = self.bank()
                for j in range(4):
                    kt = kq * 4 + j
                    self.tr(b.r(j * 128, (j + 1) * 128), self.xT.r(kt * TT + s * 128, kt * TT + (s + 1) * 128), self.ident)
                lo = s * D + kq * 512
                self.cp(xo.r(lo, lo + 512), b.r(0, 512), eng=("act" if kq % 2 == 0 else "dve"))
            self.dma("sp", dst[:, s, :], xo3[:, s, :], ("xout", s), reads=[xo.r(s * D, (s + 1) * D)])
        A.reset(m)

    def rmsnorm_fm(self, l, gcol):
        TT, A = self.TT, self.A
        m = A.mark()
        sq = A.alloc(KT * TT, BF16)
        b, _ = self.bank()
        for kt in range(KT):
            self.act(sq.r(kt * TT, (kt + 1) * TT), self.xT.r(kt * TT, (kt + 1) * TT), AF.Square)
            self.mm(b.r(0, TT), self.ones_bf.r(), sq.r(kt * TT, (kt + 1) * TT), start=(kt == 0), stop=(kt == KT - 1))
        rstd = A.alloc(TT, F32).r()
        self.rsqrt_act(rstd, b.r(0, TT), 1.0 / D, self.cst.r(0, 1))
        tmps = [A.alloc(TT, F32) for _ in range(2)]
        for kt in range(KT):
            t_ = tmps[kt % 2].r()
            self.tt(t_, self.xT.r(kt * TT, (kt + 1) * TT), rstd, ALU.mult)
            self.act(self.hT.r(kt * TT, (kt + 1) * TT), t_, AF.Identity, scale=self.pvc(l, gcol + kt))
        A.reset(m)

    def ffn(self, l, gcol, part, tag):
        TT, A = self.TT, self.A
        self.rmsnorm_fm(l, gcol)
        m = A.mark()
        HT = A.alloc(FT * TT, BF16)
        FC = 256
        nfc = DFF // FC
        wbuf = [[A.alloc(KT * FC, BF16) for _ in range(2)] for _ in range(2)]
        sg = [A.alloc(TT, F32) for _ in range(2)]
        for fc in range(nfc):
            slot = fc % 2
            for mi in range(2):
                wb = wbuf[mi][slot]
                key = (part + ("_g", "_u")[mi], l)
                if self.first:
                    src = self.dr[part + ("_w_gate", "_w_up")[mi]][l].rearrange("(k p) f -> p k f", p=128)
                    self.dma("pool", wb.ap.rearrange("p (k f) -> p k f", k=KT), src[:, :, fc * FC:(fc + 1) * FC],
                             ("w", tag, mi, slot), writes=[wb.r()])
                    self.dma("sp", self.sc[key][fc], wb.ap, ("ws", tag, mi, slot), reads=[wb.r()], writes=[self.dreg(key, fc)])
                else:
                    self.dma("sp", wb.ap, self.sc[key][fc], ("w", tag, mi, slot), reads=[self.dreg(key, fc)], writes=[wb.r()])
            for fi in range(FC // 128):
                ft = fc * (FC // 128) + fi
                (bg, _), (bu, _) = self.bank(), self.bank()
                for (mi, bb) in ((0, bg), (1, bu)):
                    for kt in range(KT):
                        self.mm(bb.r(0, TT), wbuf[mi][slot].v((KT, FC), (kt, slice(fi * 128, (fi + 1) * 128))),
                                self.hT.r(kt * TT, (kt + 1) * TT), start=(kt == 0), stop=(kt == KT - 1))
                sgt = sg[ft % 2]
                self.act(sgt.r(), bg.r(0, TT), AF.Silu)
                self.tt(HT.r(ft * TT, (ft + 1) * TT), sgt.r(), bu.r(0, TT), ALU.mult)
        DC2, FH = 256, FT // 2
        wdb = [A.alloc(FH * DC2, BF16) for _ in range(2)]
        key = (part + "_d", l)
        for dc2 in range(D // DC2):
            bos = [self.bank()[0], self.bank()[0]]
            for fh in range(2):
                idx = dc2 * 2 + fh
                wb = wdb[idx % 2]
                if self.first:
                    src = self.dr[part + "_w_down"][l].rearrange("(f p) d -> p f d", p=128)
                    self.dma("pool", wb.ap.rearrange("p (f d) -> p f d", f=FH), src[:, fh * FH:(fh + 1) * FH, dc2 * DC2:(dc2 + 1) * DC2],
                             ("wd", tag, idx % 2), writes=[wb.r()])
                    self.dma("sp", self.sc[key][idx], wb.ap, ("wds", tag, idx % 2), reads=[wb.r()], writes=[self.dreg(key, idx)])
                else:
                    self.dma("sp", wb.ap, self.sc[key][idx], ("wd", tag, idx % 2), reads=[self.dreg(key, idx)], writes=[wb.r()])
                for di in range(2):
                    for f in range(FH):
                        ft = fh * FH + f
                        self.mm(bos[di].r(0, TT), wb.v((FH, DC2), (f, slice(di * 128, (di + 1) * 128))),
                                HT.r(ft * TT, (ft + 1) * TT), start=(ft == 0), stop=(ft == FT - 1))
            for di in range(2):
                dt_ = dc2 * 2 + di
                xs = self.xT.r(dt_ * TT, (dt_ + 1) * TT)
                hs = sg[di].r()
                self.act(hs, bos[di].r(0, TT), AF.Identity, scale=0.5)
                self.tt(xs, xs, hs, ALU.add)
        A.reset(m)

    def proj(self, l, chunks, jobs, tag):
        for _ in self.proj_gen(l, chunks, jobs, tag):
            pass

    def proj_gen(self, l, chunks, jobs, tag):
        TT, A = self.TT, self.A
        m = A.mark()
        wb = [A.alloc(KT * 256, BF16) for _ in range(2)]
        for i, ch in enumerate(chunks):
            w = wb[i % 2]
            if self.first:
                src = self.dr["win"][l].rearrange("(k p) c -> p k c", p=128)
                self.dma("pool", w.ap.rearrange("p (k c) -> p k c", k=KT), src[:, :, ch * 256:(ch + 1) * 256],
                         ("win", tag, i % 2), writes=[w.r()])
                self.dma("sp", self.sc[("win", l)][ch], w.ap, ("wins", tag, i % 2), reads=[w.r()], writes=[self.dreg(("win", l), ch)])
            else:
                self.dma("sp", w.ap, self.sc[("win", l)][ch], ("win", tag, i % 2), reads=[self.dreg(("win", l), ch)], writes=[w.r()])
            for (c0, M, cb, tok) in jobs[ch]:
                if not tok:
                    b, _ = self.bank()
                    for kt in range(KT):
                        self.mm(b.r(0, TT, 0, M), w.v((KT, 256), (kt, slice(c0, c0 + M))),
                                self.hT.r(kt * TT, (kt + 1) * TT), start=(kt == 0), stop=(kt == KT - 1))
                    cb(b)
                else:
                    for s in range(TT // 128):
                        b, _ = self.bank()
                        for kt in range(KT):
                            self.mm(b.r(0, M), self.hT.r(kt * TT + s * 128, kt * TT + (s + 1) * 128),
                                    w.v((KT, 256), (kt, slice(c0, c0 + M))), start=(kt == 0), stop=(kt == KT - 1))
                        cb(b, s)
            yield
        A.reset(m)

    def wout_part(self, l, mi, yT, tag):
        TT, A = self.TT, self.A
        m = A.mark()
        wb = [A.alloc(4 * 512, BF16) for _ in range(2)]
        for dq in range(4):
            w = wb[dq % 2]
            if self.first:
                src = self.dr["wout"][l][mi * 512:(mi + 1) * 512, :].rearrange("(k p) d -> p k d", p=128)
                self.dma("pool", w.ap.rearrange("p (k d) -> p k d", k=4), src[:, :, dq * 512:(dq + 1) * 512],
                         ("wo", tag, dq % 2), writes=[w.r()])
                self.dma("sp", self.sc[("wout", l)][mi * 4 + dq], w.ap, ("wos", tag, dq % 2), reads=[w.r()],
                         writes=[self.dreg(("wout", l), mi * 4 + dq)])
            else:
                self.dma("sp", w.ap, self.sc[("wout", l)][mi * 4 + dq], ("wo", tag, dq % 2),
                         reads=[self.dreg(("wout", l), mi * 4 + dq)], writes=[w.r()])
            for di in range(4):
                dt_ = dq * 4 + di
                b, _ = self.bank()
                for kt in range(4):
                    self.mm(b.r(0, TT), w.v((4, 512), (kt, slice(di * 128, (di + 1) * 128))),
                            yT.r(kt * TT, (kt + 1) * TT), start=(kt == 0), stop=(kt == 3))
                xs = self.xT.r(dt_ * TT, (dt_ + 1) * TT)
                self.tt(xs, b.r(0, TT), xs, ALU.add)
        A.reset(m)

    def chunk_engine(self, opd, S, Sb, gL, oT, dplr):
        A = self.A
        m0 = A.mark()
        L = 64
        nch = self.TT // L
        nop = 4 if dplr else 2
        NB = 4
        outs, ints = [], []
        for i in range(NB):
            d = dict(tok=A.alloc(4 * nop * 128, BF16), ArkT=A.alloc(512, BF16))
            if dplr:
                d.update(ArbT=A.alloc(512, BF16), Gb=A.alloc(512, BF16), WaT=A.alloc(256, BF16), M1=A.alloc(512, BF16))
            outs.append(d)
        for i in range(2):
            ints.append(dict(AakT=A.alloc(512, BF16), PQ=[A.alloc(512, BF16) for _ in range(4)], G=A.alloc(512, F32)) if dplr else {})
        U = A.alloc(4 * 128, BF16) if dplr else None
        su = bc(self.msk.r(0, 64, 0, 64), (64, 4, 64), 1)
        iu = bc(self.msk.r(64, 128, 0, 64), (64, 4, 64), 1)
        sl = bc(self.msk.r(128, 192, 0, 64), (64, 4, 64), 1)
        id64 = bc(self.ident.r(0, 64, 0, 64), (64, 8, 64), 1)
        names = ["v", "kS"] + (["aI", "bS"] if dplr else [])
        h8 = lambda t, h: t.v((8, 64), (h, slice(None)), 0, 64)
        all8 = lambda t: t.v((8, 64), (slice(None), slice(None)), 0, 64)
        pair = lambda t, m: t.v((2, 4, 64), (slice(None), m, slice(None)), 0, 64)

        def phaseA(c, o, it):
            cs_ = slice(c * L, (c + 1) * L)
            tk = o["tok"]
            tkv = lambda m, q, lo=0, hi=128: tk.v((4, nop, 128), (m, q, slice(lo, hi)), 0, 64)
            for mp in range(2):
                _, bb = self.bank()
                for mm_ in range(2):
                    m = mp * 2 + mm_
                    for q, nm in enumerate(names):
                        self.tr(bb.v((2, nop, 128), (mm_, q, slice(None)), 0, 64),
                                opd[nm].v((4, 512), (m, cs_)), self.ident_bf)
                self.cp(tk.v((4, nop, 128), (slice(mp * 2, mp * 2 + 2), slice(None), slice(None)), 0, 64),
                        bb.v((2, nop, 128), (slice(None), slice(None), slice(None)), 0, 64),
                        eng=("act" if mp == 0 else "dve"))
            yield
            def amat(lname, rname, dst, msk_):
                for hh in range(2):
                    b, _ = self.bank()
                    p0 = hh * 64
                    for m in range(4):
                        self.mm(b.v((4, 64), (m, slice(None)), 0, 64), opd[lname].v((4, 512), (m, cs_), p0, p0 + 64),
                                opd[rname].v((4, 512), (m, cs_), p0, p0 + 64))
                    self.tt(dst.v((8, 64), (slice(hh * 4, hh * 4 + 4), slice(None)), 0, 64),
                            b.v((4, 64), (slice(None), slice(None)), 0, 64), msk_, ALU.mult)
            amat("kI", "rI", o["ArkT"], iu)
            yield
            if not dplr:
                return
            AakT, G, Gb, WaT, M1 = it["AakT"], it["G"], o["Gb"], o["WaT"], o["M1"]
            P, Q, Pn, Qn = it["PQ"]
            amat("bI", "rI", o["ArbT"], iu)
            yield
            amat("kI", "aI", AakT, su)
            yield
            amat("bI", "aI", P, su)
            yield
            amat("aI", "bI", Q, sl)
            self.tt(all8(G), all8(P), id64, ALU.add, eng="pool")
            self.cp(all8(Gb), all8(G), eng="pool")
            yield
            for lev in range(5):
                last = lev == 4
                bq, _ = self.bank()
                if not last:
                    bp, _ = self.bank()
                for h in range(8):
                    if not last:
                        self.mm(h8(bp, h), h8(Q, h), h8(P, h))
                    self.mm(h8(bq, h), h8(P, h), h8(Q, h))
                if not last:
                    self.cp(all8(Pn), all8(bp), eng="act")
                self.cp(all8(Qn), all8(bq), eng="dve")
                yield
                bg, _ = self.bank()
                for h in range(8):
                    self.mm(h8(bg, h), h8(Qn, h), h8(Gb, h))
                self.tt(all8(G), all8(bg), all8(G), ALU.add)
                self.cp(all8(Gb), all8(G), eng=("act" if lev % 2 else "pool"))
                P, Q, Pn, Qn = Pn, Qn, P, Q
                yield
            bw, _ = self.bank()
            for m in range(4):
                self.mm(bw.v((4, 128), (m, slice(None))), tkv(m, 2), pair(Gb, m))
            for hh in range(2):
                p0 = hh * 64
                self.cp(WaT.v((4, 64), (slice(None), slice(None)), p0, p0 + 64),
                        bw.v((4, 128), (slice(None), slice(p0, p0 + 64)), p0, p0 + 64),
                        eng=("act" if hh == 0 else "dve"))
            yield
            bm, _ = self.bank()
            for h in range(8):
                hh, m = h // 4, h % 4
                self.mm(h8(bm, h), h8(AakT, h), tkv(m, 0, hh * 64, hh * 64 + 64))
            self.cp(all8(M1), all8(bm), eng="act")
            yield

        def phaseB(c, o):
            cs_ = slice(c * L, (c + 1) * L)
            tk = o["tok"]
            tkv = lambda m, q, lo=0, hi=128: tk.v((4, nop, 128), (m, q, slice(lo, hi)), 0, 64)
            if dplr:
                Gb, WaT, M1 = o["Gb"], o["WaT"], o["M1"]
                bu, _ = self.bank()
                for m in range(4):
                    self.mm(bu.v((4, 128), (m, slice(None)), 0, 64), WaT.v((4, 64), (m, slice(None))),
                            Sb.v((4, 128), (m, slice(None))), start=True, stop=False)
                    for hh in range(2):
                        h = hh * 4 + m
                        self.mm(bu.v((4, 128), (m, slice(hh * 64, hh * 64 + 64)), 0, 64), h8(Gb, h), h8(M1, h),
                                start=False, stop=True)
                self.cp(U.v((4, 128), (slice(None), slice(None)), 0, 64),
                        bu.v((4, 128), (slice(None), slice(None)), 0, 64), eng="dve")
                yield
            by, _ = self.bank()
            for m in range(4):
                rS = opd["rS"].v((4, 512), (m, cs_))
                self.mm(by.v((4, 128), (m, slice(None))), Sb.v((4, 128), (m, slice(None))),
                        bc(rS, (128, 2, 64), 1), start=True, stop=False)
                self.mm(by.v((4, 128), (m, slice(None))), tkv(m, 0), pair(o["ArkT"], m), start=False, stop=not dplr)
                if dplr:
                    self.mm(by.v((4, 128), (m, slice(None))), U.v((4, 128), (m, slice(None)), 0, 64), pair(o["ArbT"], m),
                            start=False, stop=True)
            for hh in range(2):
                p0 = hh * 64
                self.cp(oT.v((4, 512), (slice(None), cs_), p0, p0 + 64),
                        by.v((4, 128), (slice(None), slice(p0, p0 + 64)), p0, p0 + 64),
                        eng=("act" if hh == 0 else "dve"))
            bs, _ = self.bank()
            for m in range(4):
                self.mm(bs.v((4, 128), (m, slice(None))), tkv(m, 1), tkv(m, 0), start=True, stop=not dplr)
                if dplr:
                    self.mm(bs.v((4, 128), (m, slice(None))), tkv(m, 3), U.v((4, 128), (m, slice(None)), 0, 64),
                            start=False, stop=True)
            for hh in range(2):
                p0 = hh * 64
                Sd = S.v((4, 128), (slice(None), slice(p0, p0 + 64)), p0, p0 + 64)
                g = bc(gL.v((4, 8), (slice(None), c), p0, p0 + 64), (64, 4, 64), 2)
                self.tt(Sd, Sd, g, ALU.mult, eng="pool")
                self.tt(Sd, bs.v((4, 128), (slice(None), slice(p0, p0 + 64)), p0, p0 + 64), Sd, ALU.add)
                self.cp(Sb.v((4, 128), (slice(None), slice(p0, p0 + 64)), p0, p0 + 64), Sd, eng="pool")
            yield

        def seqB(cs):
            for c in cs:
                yield from phaseB(c, outs[c % NB])

        def drive(gens):
            gens = list(gens)
            while gens:
                for g in list(gens):
                    try:
                        next(g)
                    except StopIteration:
                        gens.remove(g)
        prevB = None
        for grp in range(nch // 2):
            c0 = 2 * grp
            gens = [phaseA(c0, outs[c0 % NB], ints[0]), phaseA(c0 + 1, outs[(c0 + 1) % NB], ints[1])]
            if prevB is not None:
                gens.append(prevB)
            drive(gens)
            prevB = seqB([c0, c0 + 1])
        drive([prevB])
        A.reset(m0)

    def head_rstd(self, src_bf, out, eps_col, scale=1.0 / 64):
        b, _ = self.bank()
        self.mm(b.r(0, self.TT), self.bones.r(), src_bf)
        self.rsqrt_act(out, b.r(0, self.TT), scale, eps_col)

    def rwkv(self, l, dr, seq_start, co=None):
        TT, A = self.TT, self.A
        st = self.st[l]
        m0 = A.mark()
        pc = lambda c, n=1, p0=0, p1=128: self.pvc(l, c, n, p0, p1)
        ya = A.alloc(4 * TT, BF16)
        bonus, g_bf = A.alloc(4 * TT, BF16), A.alloc(4 * TT, BF16)
        gL = A.alloc(32, F32)
        oT = A.alloc(4 * TT, F32)
        mX = A.mark()
        opn = ["rI", "aI", "bI", "kI", "kS", "bS", "v"]
        opd = {n: A.alloc(4 * TT, BF16) for n in opn}
        opd["rS"] = opd["rI"]
        m1 = A.mark()
        bufs = [A.alloc(4 * (TT + 1), F32) for _ in range(3)]
        bufwa, bufg = A.alloc(TT + 1, F32), A.alloc(TT + 1, F32)
        waup, gup = A.alloc(512, BF16), A.alloc(512, BF16)
        self.dma("pool", waup.ap, dr["waup"][l], ("rwp", 0), writes=[waup.r()])
        self.dma("pool", gup.ap, dr["gup"][l], ("rwp", 1), writes=[gup.r()])
        if seq_start:
            self.memset(st["rwS"].r(), 0.0)
            self.memset(st["rwSb"].r(), 0.0)
            self.memset(st["rwC"].r(), 0.0)
        bv = lambda i, m, lo=1, hi=TT + 1: bufs[i].v((4, TT + 1), (m, slice(lo, hi)))
        for i in range(3):
            self.cp(bufs[i].v((4, TT + 1), (slice(None), 0)), st["rwC"].r(i * 4, i * 4 + 4), eng="pool")
        self.cp(bufwa.r(0, 1), st["rwC"].r(12, 13), eng="pool")
        self.cp(bufg.r(0, 1), st["rwC"].r(13, 14), eng="pool")
        jobs = {}
        for blk in range(14):
            if blk < 12:
                dst = bv(blk // 4, blk % 4)
            else:
                dst = (bufwa if blk == 12 else bufg).r(1, TT + 1)
            jobs.setdefault(blk // 2, []).append(((blk % 2) * 128, 128,
                                                  (lambda b, dst=dst, blk=blk: self.cp(dst, b.r(0, TT), eng=("act" if blk % 2 else "dve"))), False))
        self.proj(l, list(range(7)), jobs, "rw")
        self.ck("rw_proj")
        cs = A.alloc(4 * TT, F32)
        tmp = A.alloc(4 * TT, F32)
        a_bf, kkn, kp, b_bf = [A.alloc(4 * TT, BF16) for _ in range(4)]
        small = A.alloc(2 * TT, BF16)
        for i in range(3):
            self.cp(st["rwC"].r(i * 4, i * 4 + 4), bufs[i].v((4, TT + 1), (slice(None), TT)), eng="pool")
        self.cp(st["rwC"].r(12, 13), bufwa.r(TT, TT + 1), eng="pool")
        self.cp(st["rwC"].r(13, 14), bufg.r(TT, TT + 1), eng="pool")
        for blk in range(14):
            if blk < 12:
                cur, prv = bv(blk // 4, blk % 4), bv(blk // 4, blk % 4, 0, TT)
            else:
                bf_ = bufwa if blk == 12 else bufg
                cur, prv = bf_.r(1, TT + 1), bf_.r(0, TT)
            t_ = tmp.r((blk % 2) * 2 * TT, (blk % 2) * 2 * TT + TT)
            t2_ = tmp.r((blk % 2) * 2 * TT + TT, (blk % 2) * 2 * TT + 2 * TT)
            self.act(t_, prv, AF.Identity, scale=pc(C_MU + blk))
            self.act(t2_, cur, AF.Identity, scale=pc(C_OMU + blk))
            self.tt(cur, t_, t2_, ALU.add)
        self.ck('rw_lerp')
        r4 = bufs[0].v((4, TT + 1), (slice(None), slice(1, TT + 1)))
        v4 = bufs[2].v((4, TT + 1), (slice(None), slice(1, TT + 1)))
        f4 = lambda t: t.v((4, TT), (slice(None), slice(None)))
        low = small
        self.act(low.r(0, TT, 0, 64), bufwa.r(1, TT + 1, 0, 64), AF.Tanh)
        self.cp(low.r(0, TT, 64, 128), bufwa.r(1, TT + 1, 64, 128), eng="dve")
        for m in range(4):
            b, _ = self.bank()
            self.mm(b.r(0, TT), waup.r(m * 128, (m + 1) * 128, 0, 64), low.r(0, TT, 0, 64))
            self.act(tmp.r(m * TT, (m + 1) * TT), b.r(0, TT), AF.Sigmoid, bias=pc(C_W0 + m))
            b, _ = self.bank()
            self.mm(b.r(0, TT), waup.r(m * 128, (m + 1) * 128, 64, 128), low.r(0, TT, 64, 128))
            self.act(a_bf.r(m * TT, (m + 1) * TT), b.r(0, TT), AF.Sigmoid, bias=pc(C_A0 + m))
        self.ts(tmp.r(), tmp.r(), -0.6065306597126334, None, ALU.mult)
        self.op("dve", lambda e: e.tensor_tensor_scan(cs.ap, self.cmask.ap, tmp.ap, 0.0, ALU.mult, ALU.add),
                [self.cmask.r(), tmp.r()], [cs.r()])
        sgl = small.r(TT, 2 * TT)
        self.act(sgl, bufg.r(1, TT + 1), AF.Sigmoid)
        for m in range(4):
            b, _ = self.bank()
            self.mm(b.r(0, TT), gup.r(m * 128, (m + 1) * 128), sgl)
            self.cp(g_bf.r(m * TT, (m + 1) * TT), b.r(0, TT), eng="act")
        self.ck('rw_low')
        self.tt(tmp.r(), cs.r(), tmp.r(), ALU.subtract)
        self.act(tmp.r(), tmp.r(), AF.Exp)
        eprev = tmp
        for m in range(4):
            t_ = small.r(0, TT)
            kf = A.alloc(TT, F32)
            self.act(kf.r(), bv(1, m), AF.Identity, scale=pc(C_KK + m))
            self.act(t_, kf.r(), AF.Square)
            b, _ = self.bank()
            self.mm(b.r(0, TT), self.bones.r(), t_)
            rs = A.alloc(TT, F32)
            self.act(rs.r(), b.r(0, TT), AF.Ln)
            self.act(rs.r(), rs.r(), AF.Exp, scale=-0.5)
            self.tt(kkn.r(m * TT, (m + 1) * TT), kf.r(), rs.r(), ALU.mult)
            self.ts(kf.r(), a_bf.r(m * TT, (m + 1) * TT), 1.0, pc(C_KA + m), ALU.subtract, ALU.mult)
            self.stt(kp.r(m * TT, (m + 1) * TT), kf.r(), 1.0, bv(1, m), ALU.add, ALU.mult)
            A.reset(A.mark() - 2 * TT)
        self.tt(b_bf.r(), kkn.r(), a_bf.r(), ALU.mult, eng="pool")
        self.stt(opd["aI"].r(), kkn.r(), -1.0, eprev.r(), ALU.mult, ALU.mult)
        self.ck('rw_kk')
        for m in range(4):
            t_ = small.r(0, TT)
            self.stt(t_, bv(0, m), pc(C_RK + m), kp.r(m * TT, (m + 1) * TT), ALU.mult, ALU.mult)
            b, _ = self.bank()
            self.mm(b.r(0, TT), self.bones.r(), t_)
            self.tt(bonus.r(m * TT, (m + 1) * TT), b.r(0, TT), bv(2, m), ALU.mult)
        self.cp(f4(opd["v"]), v4, eng="pool")
        self.act(gL.r(), cs.v((4 * TT // 64, 64), (slice(None), 63)), AF.Exp)
        self.act(tmp.r(), cs.r(), AF.Exp)
        self.tt(f4(opd["rI"]), r4, f4(tmp), ALU.mult)
        self.act(tmp.r(), cs.r(), AF.Exp, scale=-1.0)
        self.tt(opd["bI"].r(), b_bf.r(), tmp.r(), ALU.mult)
        self.tt(opd["kI"].r(), kp.r(), tmp.r(), ALU.mult, eng="pool")
        nseg = 4 * TT // 64
        cL = bc(cs.v((nseg, 64), (slice(None), 63)), (128, nseg, 64), 2)
        self.tt(tmp.v((nseg, 64), (slice(None), slice(None))), cL, cs.v((nseg, 64), (slice(None), slice(None))), ALU.subtract)
        self.act(tmp.r(), tmp.r(), AF.Exp)
        self.tt(opd["kS"].r(), kp.r(), tmp.r(), ALU.mult)
        self.tt(opd["bS"].r(), b_bf.r(), tmp.r(), ALU.mult, eng="pool")
        A.reset(m1)
        self.ck("rw_ops")
        self.chunk_engine(opd, st["rwS"], st["rwSb"], gL, oT, True)
        self.ck("rw_ce")
        A.reset(mX)
        otmp = (A.alloc(TT, BF16), A.alloc(TT, F32), A.alloc(TT, F32))

        def step_co(k):
            if co is not None:
                for _ in range(k):
                    next(co, None)
        for m in range(4):
            sl_ = lambda t: t.r(m * TT, (m + 1) * TT)
            obf, cen, rs = otmp
            self.cp(obf.r(), sl_(oT), eng="act")
            b, _ = self.bank()
            self.mm(b.r(0, TT), self.bones.r(), obf.r())
            self.stt(cen.r(), b.r(0, TT), -1.0 / 64, sl_(oT), ALU.mult, ALU.add)
            self.act(obf.r(), cen.r(), AF.Square)
            self.head_rstd(obf.r(), rs.r(), self.cst.r(1, 2))
            self.tt(cen.r(), cen.r(), rs.r(), ALU.mult)
            self.ts(cen.r(), cen.r(), pc(C_LNW + m), pc(C_LNB + m), ALU.mult, ALU.add)
            self.tt(cen.r(), cen.r(), sl_(bonus), ALU.add)
            self.tt(sl_(ya), cen.r(), sl_(g_bf), ALU.mult)
            step_co(2)
        self.wout_part(l, 0, ya, "rw")
        if co is not None:
            for _ in co:
                pass
        A.reset(m0)

    def hgrn(self, l, dr, seq_start):
        for _ in self.hgrn_gen(l, dr, seq_start):
            pass

    def hgrn_gen(self, l, dr, seq_start):
        TT, A = self.TT, self.A
        st = self.st[l]
        m0 = A.mark()
        pc = lambda c, n=1, p0=0, p1=128: self.pvc(l, c, n, p0, p1)
        yd = A.alloc(4 * TT, BF16)
        opd = {n: A.alloc(4 * TT, BF16) for n in ["rI", "kI", "rS", "kS", "v"]}
        sg_bf = A.alloc(4 * TT, BF16)
        gL = A.alloc(32, F32)
        m1 = A.mark()
        q_bf, k_bf = A.alloc(4 * TT, BF16), A.alloc(4 * TT, BF16)
        ld, cs, tmp = A.alloc(4 * TT, F32), A.alloc(4 * TT, F32), A.alloc(4 * TT, F32)
        if seq_start:
            self.memset(st["hgS"].r(), 0.0)
            self.memset(st["hgSb"].r(), 0.0)
        jobs = {}
        for blk in range(16):
            kind, m = blk // 4, blk % 4
            sl_ = lambda t, m=m: t.r(m * TT, (m + 1) * TT)
            if kind == 0:
                cb = lambda b, sl_=sl_: self.act(sl_(q_bf), b.r(0, TT), AF.Silu)
            elif kind == 1:
                def cb(b, sl_=sl_, m=m):
                    self.act(sl_(tmp), b.r(0, TT), AF.Sigmoid)
                    self.ts(sl_(tmp), sl_(tmp), pc(C_OLB + m), pc(C_LB + m), ALU.mult, ALU.add)
                    self.act(sl_(ld), sl_(tmp), AF.Ln)
                    self.ts(sl_(k_bf), sl_(tmp), -1.0, 1.0, ALU.mult, ALU.add)
            elif kind == 2:
                cb = lambda b, sl_=sl_: self.cp(sl_(opd["v"]), b.r(0, TT), eng="dve")
            else:
                cb = lambda b, sl_=sl_: self.act(sl_(sg_bf), b.r(0, TT), AF.Silu)
            jobs.setdefault(7 + blk // 2, []).append(((blk % 2) * 128, 128, cb, False))
        yield from self.proj_gen(l, list(range(7, 15)), jobs, "hg")
        self.ck("hg_proj")
        self.op("dve", lambda e: e.tensor_tensor_scan(cs.ap, self.cmask.ap, ld.ap, 0.0, ALU.mult, ALU.add),
                [self.cmask.r(), ld.r()], [cs.r()])
        self.ck('hg_scan')
        nseg = 4 * TT // 64
        seg = lambda t: t.v((nseg, 64), (slice(None), slice(None)))
        cm = bc(cs.v((nseg, 64), (slice(None), 31)), (128, nseg, 64), 2)
        cL = bc(cs.v((nseg, 64), (slice(None), 63)), (128, nseg, 64), 2)
        self.act(gL.r(), cs.v((nseg, 64), (slice(None), 63)), AF.Exp)
        self.tt(seg(ld), seg(cs), cm, ALU.subtract)
        self.act(tmp.r(), ld.r(), AF.Exp)
        self.stt(opd["rI"].r(), q_bf.r(), 0.125, tmp.r(), ALU.mult, ALU.mult)
        self.act(tmp.r(), ld.r(), AF.Exp, scale=-1.0)
        self.tt(opd["kI"].r(), k_bf.r(), tmp.r(), ALU.mult, eng="pool")
        self.act(tmp.r(), cs.r(), AF.Exp)
        self.stt(opd["rS"].r(), q_bf.r(), 0.125, tmp.r(), ALU.mult, ALU.mult)
        self.tt(seg(ld), cL, seg(cs), ALU.subtract)
        self.act(tmp.r(), ld.r(), AF.Exp)
        self.tt(opd["kS"].r(), k_bf.r(), tmp.r(), ALU.mult, eng="pool")
        yield
        A.reset(m1)
        oT = A.alloc(4 * TT, F32)
        self.ck("hg_ops")
        self.chunk_engine(opd, st["hgS"], st["hgSb"], gL, oT, False)
        self.ck("hg_ce")
        for m in range(4):
            sl_ = lambda t: t.r(m * TT, (m + 1) * TT)
            sq, rs = A.alloc(TT, BF16), A.alloc(TT, F32)
            self.act(sq.r(), sl_(oT), AF.Square)
            self.head_rstd(sq.r(), rs.r(), self.cst.r(0, 1))
            self.stt(rs.r(), sl_(oT), pc(C_GN), rs.r(), ALU.mult, ALU.mult)
            self.tt(sl_(yd), rs.r(), sl_(sg_bf), ALU.mult)
            A.reset(A.mark() - (TT // 2 + TT))
        self.wout_part(l, 3, yd, "hg")
        A.reset(m0)

    def s5(self, l, dr, seq_start):
        for _ in self.s5_gen(l, dr, seq_start):
            pass

    def s5_gen(self, l, dr, seq_start):
        TT, A = self.TT, self.A
        st = self.st[l]
        m0 = A.mark()
        pc = lambda c, n=1, p0=0, p1=128: self.pvc(l, c, n, p0, p1)
        yc = A.alloc(4 * TT, BF16)
        u32, ubf = A.alloc(4 * TT, F32), A.alloc(4 * TT, BF16)
        Bbd = A.alloc(4096, BF16)
        Cbd = A.alloc(4096, BF16)
        gluw = A.alloc(4 * 512, BF16)
        t1 = A.alloc(128, F32)
        mcf = A.mark()
        Cf = A.alloc(4096, F32)
        self.dma("pool", Bbd.ap, dr["s5b"][l], ("s5p", 0), writes=[Bbd.r()])
        self.dma("sp", Cf.ap, dr["s5c"][l], ("s5p", 1), writes=[Cf.r()])
        self.dma("pool", gluw.ap.rearrange("p (k c) -> p k c", k=4), dr["gluw"][l].rearrange("(k p) c -> p k c", p=128),
                 ("s5p", 2), writes=[gluw.r()])
        if seq_start:
            self.memset(st["s5h"].r(), 0.0)
        jobs = {}
        for blk in range(4):
            def cb(b, blk=blk):
                self.cp(u32.r(blk * TT, (blk + 1) * TT), b.r(0, TT), eng="act")
                self.cp(ubf.r(blk * TT, (blk + 1) * TT), b.r(0, TT), eng="dve")
            jobs.setdefault(15 + blk // 2, []).append(((blk % 2) * 128, 128, cb, False))
        self.proj(l, [15, 16], jobs, "s5")
        yield
        cv = lambda t, ri, s: t.v((2, 16, 128), (ri, s, slice(None)))
        for s in range(16):
            self.ts(t1.r(), cv(Cf, 0, s), pc(C_KRE + s), None, ALU.mult)
            self.stt(cv(Cbd, 0, s), cv(Cf, 1, s), pc(C_NKIM + s), t1.r(), ALU.mult, ALU.add)
            self.ts(t1.r(), cv(Cf, 0, s), pc(C_NKIM + s), None, ALU.mult)
            self.stt(cv(Cbd, 1, s), cv(Cf, 1, s), pc(C_NKRE + s), t1.r(), ALU.mult, ALU.add)
        yield
        A.reset(mcf)
        sets = []
        for _ in range(2):
            d = {n_: A.alloc(TT, F32) for n_ in ("sn", "cs", "ta", "tb", "tc", "td", "xre", "xim", "gre", "gim")}
            d["ang"] = d["ta"]
            d["kis"] = A.alloc(TT, I32)
            d["hre"], d["him"] = A.alloc(TT, BF16), A.alloc(TT, BF16)
            sets.append(d)
        yb = {}

        def st_gen(s):
            q4 = s // 4
            B_ = sets[s % 2]
            ang, sn, cs_, ta, tb, tc, td, xre, xim, gre, gim, kis, hre, him = [B_[n_] for n_ in (
                "ang", "sn", "cs", "ta", "tb", "tc", "td", "xre", "xim", "gre", "gim", "kis", "hre", "him")]
            if s % 4 == 0:
                yb[q4] = self.reserve()
            ybank = yb[q4]
            (bre, _), (bim, _) = self.bank(), self.bank()
            self.mm(bre.r(0, TT), cv(Bbd, 0, s), ubf.r(q4 * TT, (q4 + 1) * TT))
            self.mm(bim.r(0, TT), cv(Bbd, 1, s), ubf.r(q4 * TT, (q4 + 1) * TT))
            self.ts(ang.r(), self.iota1.r(), pc(C_TH + s), None, ALU.mult)
            self.sincos(sn.r(), cs_.r(), ang.r(), xre.r(), kis.r())
            yield
            self.tt(ta.r(), bre.r(0, TT), cs_.r(), ALU.mult)
            self.tt(tb.r(), bim.r(0, TT), sn.r(), ALU.mult)
            self.tt(tc.r(), bim.r(0, TT), cs_.r(), ALU.mult)
            self.tt(td.r(), bre.r(0, TT), sn.r(), ALU.mult)
            self.tt(xre.r(), ta.r(), tb.r(), ALU.add, eng="pool")
            self.tt(xim.r(), tc.r(), td.r(), ALU.subtract, eng="pool")
            yield
            rmb = R(pc(C_RM + s).ap.broadcast_to([128, TT]), pc(C_RM + s).region)
            hr0, hi0 = st["s5h"].r(s, s + 1), st["s5h"].r(16 + s, 17 + s)
            self.op("dve", lambda e, rmb=rmb, hr0=hr0, gre=gre, xre=xre: e.tensor_tensor_scan(gre.ap, rmb.ap, xre.ap, hr0.ap, ALU.mult, ALU.add),
                    [rmb, xre.r(), hr0], [gre.r()])
            self.op("dve", lambda e, rmb=rmb, hi0=hi0, gim=gim, xim=xim: e.tensor_tensor_scan(gim.ap, rmb.ap, xim.ap, hi0.ap, ALU.mult, ALU.add),
                    [rmb, xim.r(), hi0], [gim.r()])
            yield
            self.tt(ta.r(), gre.r(), cs_.r(), ALU.mult, eng="pool")
            self.tt(tb.r(), gim.r(), sn.r(), ALU.mult, eng="pool")
            self.tt(tc.r(), gim.r(), cs_.r(), ALU.mult, eng="pool")
            self.tt(td.r(), gre.r(), sn.r(), ALU.mult, eng="pool")
            yield
            self.tt(hre.r(), ta.r(), tb.r(), ALU.subtract)
            self.tt(him.r(), tc.r(), td.r(), ALU.add)
            c5, s5_ = cs_.r(TT - 1, TT), sn.r(TT - 1, TT)
            t2 = t1.r(s % 2, s % 2 + 1)
            t3 = t1.r(2 + s % 2, 3 + s % 2)
            self.ts(t2, gim.r(TT - 1, TT), s5_, None, ALU.mult)
            self.ts(t3, gre.r(TT - 1, TT), s5_, None, ALU.mult)
            self.stt(hr0, gre.r(TT - 1, TT), c5, t2, ALU.mult, ALU.subtract)
            self.stt(hi0, gim.r(TT - 1, TT), c5, t3, ALU.mult, ALU.add)
            self.mm(ybank.r(0, TT), cv(Cbd, 0, s), hre.r(), start=(s % 4 == 0), stop=False)
            self.mm(ybank.r(0, TT), cv(Cbd, 1, s), him.r(), start=False, stop=(s % 4 == 3))
            if s % 4 == 3:
                us = u32.r(q4 * TT, (q4 + 1) * TT)
                self.stt(us, us, pc(C_DSK + q4), ybank.r(0, TT), ALU.mult, ALU.add)
                self.act(us, us, AF.Gelu)
                self.cp(ubf.r(q4 * TT, (q4 + 1) * TT), us, eng="pool")
                self.release(ybank)

        for s0 in range(0, 16, 2):
            gens = [st_gen(s0), st_gen(s0 + 1)]
            while gens:
                for g_ in list(gens):
                    try:
                        next(g_)
                    except StopIteration:
                        gens.remove(g_)
                yield
        for q in range(4):
            b, _ = self.bank()
            for kt in range(4):
                self.mm(b.r(0, TT), gluw.v((4, 512), (kt, slice(q * 128, (q + 1) * 128))), ubf.r(kt * TT, (kt + 1) * TT),
                        start=(kt == 0), stop=(kt == 3))
            tq = sets[q % 2]["ta"]
            self.act(tq.r(), b.r(0, TT), AF.Sigmoid, bias=pc(C_GLB + q))
            self.tt(yc.r(q * TT, (q + 1) * TT), u32.r(q * TT, (q + 1) * TT), tq.r(), ALU.mult)
        yield
        self.wout_part(l, 2, yc, "s5")
        A.reset(m0)

    def attn(self, l, dr, seq_start, pos_ap, co=None):
        TT, A = self.TT, self.A
        st = self.st[l]
        m0 = A.mark()
        pc = lambda c, n=1, p0=0, p1=128: self.pvc(l, c, n, p0, p1)
        yb = A.alloc(4 * TT, BF16)
        qT = A.alloc(10 * TT, BF16)
        m1 = A.mark()
        qraw, qsw = A.alloc(10 * TT, F32), A.alloc(10 * TT, F32)
        vaug = st["vaug"]
        VA = lambda blk, g, lo=0, hi=65: vaug.v((5, 2, 65), (blk, g, slice(lo, hi)))
        posi = A.alloc(TT, I32)
        kia = A.alloc(TT, I32)
        posf, ang, sn, cs_, tmp, tA, tB = [A.alloc(TT, F32) for _ in range(7)]
        P16 = dict(p0=0, p1=16)
        self.dma("sp", posi.ap[0:16, :], pos_ap.partition_broadcast(16), ("pos", 0), writes=[posi.r(0, TT, 0, 16)])
        self.cp(posf.r(0, TT, **P16), posi.r(0, TT, **P16), eng="dve")
        self.ts(ang.r(0, TT, **P16), posf.r(0, TT, **P16), pc(C_INVF, 1, 0, 16), None, ALU.mult)
        self.sincos(sn.r(0, TT, **P16), cs_.r(0, TT, **P16), ang.r(0, TT, **P16), tmp.r(0, TT, **P16), kia.r(0, TT, **P16))
        self.ts(sn.r(0, TT, **P16), sn.r(0, TT, **P16), pc(C_SGN, 1, 0, 16), None, ALU.mult)
        self.memset(vaug.v((5, 2, 65), (slice(None), slice(None), 64)), 1.0)
        jobs = {17: [(0, 128, lambda b, s: self.cp(vaug.v((5, 2, 65), (1 + s, slice(None), slice(0, 64))),
                                                   b.v((2, 64), (slice(None), slice(None))), eng="act"), True)]}
        hq = lambda t, h, p1=64: t.v((10, TT), (h, slice(None)), 0, p1)
        for h in range(10):
            ch, c0 = (18 + h // 4, (h % 4) * 64) if h < 8 else (20, (h - 8) * 64)
            jobs.setdefault(ch, []).append((c0, 64, (lambda b, h=h: self.cp(hq(qraw, h), b.r(0, TT, 0, 64), eng="act")), False))
        for h in range(10):
            ch, c0 = (20, 128 + h * 16) if h < 8 else (21, (h - 8) * 16)
            jobs.setdefault(ch, []).append((c0, 16, (lambda b, h=h: self.cp(hq(qsw, h, 16), b.r(0, TT, 0, 16), eng="dve")), False))
        self.proj(l, [17, 18, 19, 20, 21], jobs, "at")
        sq = A.alloc(TT, BF16)
        rs = A.alloc(TT, F32)
        for h in range(10):
            gcol, gscol = (C_QN, C_QNS) if h < 8 else (C_KN, C_KNS)
            self.act(sq.r(0, TT, 0, 64), hq(qraw, h), AF.Square)
            b, _ = self.bank()
            self.mm(b.r(0, TT, 0, 64), self.ones_bf.r(0, 64, 0, 64), sq.r(0, TT, 0, 64))
            self.rsqrt_act(rs.r(0, TT, 0, 64), b.r(0, TT, 0, 64), 1.0 / 64, self.cst.r(0, 1, 0, 64))
            self.stt(hq(qT, h), hq(qraw, h), pc(gcol, 1, 0, 64), rs.r(0, TT, 0, 64), ALU.mult, ALU.mult)
            self.stt(tA.r(0, TT, **P16), hq(qraw, h, 16), pc(gcol, 1, 0, 16), rs.r(0, TT, **P16), ALU.mult, ALU.mult)
            self.stt(tB.r(0, TT, **P16), hq(qsw, h, 16), pc(gscol, 1, 0, 16), rs.r(0, TT, **P16), ALU.mult, ALU.mult)
            self.tt(tA.r(0, TT, **P16), tA.r(0, TT, **P16), cs_.r(0, TT, **P16), ALU.mult)
            self.tt(tB.r(0, TT, **P16), tB.r(0, TT, **P16), sn.r(0, TT, **P16), ALU.mult)
            self.tt(hq(qT, h, 16), tA.r(0, TT, **P16), tB.r(0, TT, **P16), ALU.add)
        A.reset(m1)
        E = [A.alloc(512, BF16) for _ in range(2)]
        Pm = [A.alloc(512, BF16) for _ in range(2)]
        ytok = A.alloc(512, F32)
        den = A.alloc(8, F32)
        mcur = bc(self.amask.r(0, 128), (128, 4, 128), 1)
        mprev = bc(self.amask.r(128, 256), (128, 4, 128), 1)
        esk = self.sinks.r(l * 8, (l + 1) * 8)

        def step_co(k):
            if co is not None:
                for _ in range(k):
                    next(co, None)
        for n in range(TT // 128):
            blk = slice(n * 128, (n + 1) * 128)
            has_prev = not (seq_start and n == 0)
            for g in range(2):
                qv = qT.v((10, TT), (slice(4 * g, 4 * g + 4), blk), 0, 64)
                srcs = [(qT.v((10, TT), (8 + g, blk), 0, 64), mcur, 1)]
                if has_prev:
                    kp = (qT.v((10, TT), (8 + g, slice((n - 1) * 128, n * 128)), 0, 64) if n > 0
                          else st["kprev"].v((2, 128), (g, slice(None)), 0, 64))
                    srcs.append((kp, mprev, 0))
                for i, (kT_, msk_, _) in enumerate(srcs):
                    b, _ = self.bank()
                    self.mm(b.v((4, 128), (slice(None), slice(None))), kT_, qv)
                    self.act(E[i].r(), b.r(0, 512), AF.Exp, scale=0.125)
                    self.tt(Pm[i].v((4, 128), (slice(None), slice(None))), E[i].v((4, 128), (slice(None), slice(None))),
                            msk_, ALU.mult, eng="pool")
                bo, _ = self.bank()
                for j in range(4):
                    o_ = bo.v((4, 65), (j, slice(None)))
                    if has_prev:
                        self.mm(o_, Pm[1].r(j * 128, (j + 1) * 128), VA(n, g), start=True, stop=False)
                    self.mm(o_, Pm[0].r(j * 128, (j + 1) * 128), VA(n + 1, g), start=not has_prev, stop=True)
                dn = den.r(4 * g, 4 * g + 4)
                self.tt(dn, bo.v((4, 65), (slice(None), 64)), self.sinks.r(l * 8 + 4 * g, l * 8 + 4 * g + 4), ALU.add)
                self.op("dve", lambda e, dn=dn: e.reciprocal(dn.ap, dn.ap), [dn], [dn])
                self.tt(ytok.v((8, 64), (slice(4 * g, 4 * g + 4), slice(None))), bo.v((4, 65), (slice(None), slice(0, 64))),
                        bc(dn, (128, 4, 64), 2), ALU.mult)
                step_co(5)
            b, _ = self.bank()
            for kt in range(4):
                self.tr(b.r(kt * 128, (kt + 1) * 128), ytok.r(kt * 128, (kt + 1) * 128), self.ident)
            self.cp(yb.v((4, TT), (slice(None), blk)), b.v((4, 128), (slice(None), slice(None))), eng="act")
            step_co(5)
        if co is not None:
            for _ in co:
                pass
        self.cp(st["kprev"].v((2, 128), (slice(None), slice(None)), 0, 64),
                qT.v((10, TT), (slice(8, 10), slice(TT - 128, TT)), 0, 64), eng="pool")
        self.cp(vaug.v((5, 2, 65), (0, slice(None), slice(None))), vaug.v((5, 2, 65), (4, slice(None), slice(None))), eng="pool")
        self.wout_part(l, 1, yb, "at")
        A.reset(m0)

    def reserve(self):
        b, _ = self.bank()
        i = self.banks.index(b)
        self.reserved = getattr(self, "reserved", set())
        self.reserved.add(i)
        return b

    def release(self, b):
        self.reserved.discard(self.banks.index(b))

    def bank(self):
        res = getattr(self, "reserved", set())
        while True:
            i = self.bank_rr % 8
            self.bank_rr += 1
            if i not in res:
                return self.banks[i], self.banks_bf[i]

    def mixer(self, l, dr, seq_start, pos_ap, which=("rwkv", "attn", "s5", "hgrn")):
        self.rmsnorm_fm(l, C_MIX)
        if "rwkv" in which and "hgrn" in which:
            self.rwkv(l, dr, seq_start, co=self.hgrn_gen(l, dr, seq_start))
        else:
            if "rwkv" in which:
                self.rwkv(l, dr, seq_start)
            if "hgrn" in which:
                self.hgrn(l, dr, seq_start)
        if "s5" in which and "attn" in which:
            self.attn(l, dr, seq_start, pos_ap, co=self.s5_gen(l, dr, seq_start))
        else:
            if "s5" in which:
                self.s5(l, dr, seq_start)
            if "attn" in which:
                self.attn(l, dr, seq_start, pos_ap)


def build_program(n_tok, seq_len, nlayers=2, stages=("ffn1", "mix", "ffn2"), which=("rwkv", "attn", "s5", "hgrn")):
    nc = bass.Bass("TRN2", target_bir_lowering=False)
    L = nlayers
    nseq = n_tok // seq_len
    shapes = dict(x=([n_tok, D], F32), pos=([nseq, seq_len], I32), win=([L, D, NCIN], F32), pv=([L, 128, NV], F32),
                  waup=([L, 128, 512], F32), gup=([L, 128, 512], F32), s5b=([L, 128, 4096], F32), s5c=([L, 128, 4096], F32),
                  gluw=([L, 512, 512], F32), wout=([L, D, D], F32), sinks=([L, 128, 8], F32),
                  ffn1_w_gate=([L, D, DFF], F32), ffn1_w_up=([L, D, DFF], F32), ffn1_w_down=([L, DFF, D], F32),
                  ffn2_w_gate=([L, D, DFF], F32), ffn2_w_up=([L, D, DFF], F32), ffn2_w_down=([L, DFF, D], F32))
    if "ffn1" not in stages:
        for k in ("ffn1_w_gate", "ffn1_w_up", "ffn1_w_down"):
            shapes.pop(k)
    if "ffn2" not in stages:
        for k in ("ffn2_w_gate", "ffn2_w_up", "ffn2_w_down"):
            shapes.pop(k)
    dr = {k: nc.dram_tensor(k, s, t, kind="ExternalInput").ap() for k, (s, t) in shapes.items()}
    y = nc.dram_tensor("y", [n_tok, D], F32, kind="ExternalOutput").ap()
    E = Emit(nc, nlayers)
    E.setup_common(dr)
    E.make_scratch(L)
    TT = E.TT
    tps = seq_len // TT
    for it in range(n_tok // TT):
        E.first = (it == 0)
        E.load_x_tile(dr["x"], it)
        sq, ti = it // tps, it % tps
        for l in range(L):
            if "ffn1" in stages:
                E.ffn(l, C_FFN1, "ffn1", "f1")
            if "mix" in stages:
                E.mixer(l, dr, ti == 0, dr["pos"][sq, ti * TT:(ti + 1) * TT], which)
            if "ffn2" in stages:
                E.ffn(l, C_FFN2, "ffn2", "f2")
        E.store_x_tile(y, it)
    E.S.emit()
    return nc, E


_PROG_CACHE = {}


def kernel(**inputs):
    inp = {k: np.asarray(v) for k, v in inputs.items()}
    B, S_, _ = inp["x"].shape
    per = B // NCORES
    packed = pack_all(inp)
    key = (per * S_, S_)
    if key not in _PROG_CACHE:
        _PROG_CACHE[key] = build_program(per * S_, S_)[0]
    nc = _PROG_CACHE[key]
    in_maps = []
    for c in range(NCORES):
        d = dict(packed)
        d["x"] = np.ascontiguousarray(inp["x"][c * per:(c + 1) * per].reshape(per * S_, D))
        d["pos"] = np.ascontiguousarray(inp["positions"][c * per:(c + 1) * per].astype(np.int32))
        in_maps.append(d)
    res = run_bass_kernel_spmd(nc, in_maps, core_ids=list(range(NCORES)))
    out = np.concatenate([np.asarray(r["y"]).reshape(per, S_, D) for r in res.results], axis=0)
    return out.astype(np.float32)
```

```python
import numpy as np
import concourse.bass as bass
import concourse.mybir as mybir
from concourse.bass_utils import run_bass_kernel_spmd

F32 = mybir.dt.float32
BF16 = mybir.dt.bfloat16
I32 = mybir.dt.int32
AF = mybir.ActivationFunctionType
ALU = mybir.AluOpType
AX = mybir.AxisListType

D = 2048
DFF = 5632
KT = D // 128
FT = DFF // 128
NCORES = 8


class Op:
    __slots__ = ("eng", "fn", "deps", "dma", "semkey", "idx", "eidx", "need", "val", "waits")


class Sched:
    ENGS = ["pe", "act", "dve", "pool", "sp"]

    def __init__(self, nc):
        self.nc = nc
        self.ops = []
        self.eng_ops = {e: [] for e in self.ENGS}
        self.regs = {}
        self.dma_count = {}

    def _overlap(self, e, r):
        return e[0] < r[2] and r[1] < e[1] and e[2] < r[4] and r[3] < e[3]

    def add(self, eng, fn, reads=(), writes=(), dma=False, semkey=None):
        op = Op()
        op.eng, op.fn, op.dma, op.semkey = eng, fn, dma, semkey
        op.idx = len(self.ops)
        deps = set()

        def norm(rg):
            if rg[0] != "ps":
                return rg
            return ("ps", rg[1] // 32 * 32, (rg[2] + 31) // 32 * 32, rg[3] // 512 * 512, (rg[4] + 511) // 512 * 512)
        reads = [norm(r) for r in reads]
        writes = [norm(w) for w in writes]
        for r in reads:
            lst = self.regs.setdefault(r[0], [])
            exact = None
            for e in lst:
                if self._overlap(e, r):
                    if e[4] is not None:
                        deps.add(e[4])
                    if r[0] == "ps":
                        for k_, v_ in e[5].items():
                            if k_ != eng:
                                deps.add(v_)
                    if e[0] == r[1] and e[1] == r[2] and e[2] == r[3] and e[3] == r[4]:
                        exact = e
            if exact is None:
                exact = [r[1], r[2], r[3], r[4], None, {}]
                lst.append(exact)
            exact[5][("d", op.idx) if dma else eng] = op.idx
        for w in writes:
            lst = self.regs.setdefault(w[0], [])
            keep = []
            for e in lst:
                if self._overlap(e, w):
                    if e[4] is not None:
                        deps.add(e[4])
                    deps.update(e[5].values())
                    if w[1] <= e[0] and e[1] <= w[2] and w[3] <= e[2] and e[3] <= w[4]:
                        continue
                keep.append(e)
            keep.append([w[1], w[2], w[3], w[4], op.idx, {}])
            self.regs[w[0]] = keep
        deps.discard(op.idx)
        op.deps = deps
        op.eidx = len(self.eng_ops[eng])
        op.need = False
        op.val = None
        if dma:
            self.dma_count[semkey] = self.dma_count.get(semkey, 0) + 1
            op.val = self.dma_count[semkey]
        self.ops.append(op)
        self.eng_ops[eng].append(op)
        return op

    def emit(self, final_wait_eng="sp"):
        nc = self.nc
        ops = self.ops
        for eng in self.ENGS:
            waited = {}
            dma_issued_view = None
            for op in self.eng_ops[eng]:
                ws = []
                latest = {}
                for d in op.deps:
                    p = ops[d]
                    if p.dma:
                        key = ("dma", p.semkey)
                        cnt = self._dma_issued_before(p.semkey, op.idx)
                        if waited.get(key, 0) < cnt:
                            waited[key] = cnt
                            ws.append((key, cnt))
                    else:
                        if p.eng not in latest or latest[p.eng].eidx < p.eidx:
                            latest[p.eng] = p
                for pe_, p in latest.items():
                    if pe_ == eng:
                        if eng == "pe":
                            continue
                        if op.eidx - p.eidx > 3:
                            continue
                    if waited.get(pe_, -1) >= p.eidx:
                        continue
                    waited[pe_] = p.eidx
                    p.need = True
                    ws.append((pe_, p))
                op.waits = ws
        for eng in self.ENGS:
            c = 0
            for op in self.eng_ops[eng]:
                if not op.dma and op.need:
                    c += 1
                    op.val = c
        import contextlib
        stack = contextlib.ExitStack()
        esem = {e: stack.enter_context(nc.semaphore("s_" + e)) for e in self.ENGS}
        dsem = {}
        for k in self.dma_count:
            dsem[k] = stack.enter_context(nc.semaphore("d_%d" % len(dsem)))
        engobj = {"pe": "tensor", "act": "scalar", "dve": "vector", "pool": "gpsimd", "sp": "sync"}
        with nc.Block() as block:
            for eng in self.ENGS:
                def body(e, eng=eng):
                    for op in self.eng_ops[eng]:
                        for (k, v) in op.waits:
                            if isinstance(k, tuple):
                                e.wait_ge(dsem[k[1]], 16 * v)
                            else:
                                e.wait_ge(esem[k], v.val)
                        ins = op.fn(e)
                        if op.dma:
                            ins.then_inc(dsem[op.semkey], 16)
                        elif op.need:
                            ins.then_inc(esem[eng], 1)
                    if eng == final_wait_eng:
                        for k, c in self.dma_count.items():
                            e.wait_ge(dsem[k], 16 * c)
                getattr(block, engobj[eng])(body)
        stack.close()

    def _dma_issued_before(self, semkey, idx):
        lst = self._dma_lists().get(semkey, [])
        import bisect
        return bisect.bisect_left(lst, idx)

    def _dma_lists(self):
        if getattr(self, "_dl_n", -1) != len(self.ops):
            dl = {}
            for op in self.ops:
                if op.dma:
                    dl.setdefault(op.semkey, []).append(op.idx)
            self._dl = dl
            self._dl_n = len(self.ops)
        return self._dl


def _prod(x):
    r = 1
    for v in x:
        r *= v
    return r


class R:
    __slots__ = ("ap", "region")

    def __init__(self, ap, region):
        self.ap, self.region = ap, region


class Tile:
    def __init__(self, space, base_ap, off, words, dtype=F32):
        self.space, self.off, self.words, self.dtype = space, off, words, dtype
        ap = base_ap[:, off:off + words]
        if dtype != F32:
            ap = ap.bitcast(dtype)
        self.ap = ap
        self.epw = 2 if dtype == BF16 else 1
        self.n = words * self.epw

    def reg(self, lo=None, hi=None, p0=0, p1=128):
        if lo is None:
            return (self.space, p0, p1, self.off, self.off + self.words)
        return (self.space, p0, p1, self.off + lo // self.epw, self.off + (hi + self.epw - 1) // self.epw)

    def r(self, lo=None, hi=None, p0=0, p1=128):
        if lo is None:
            lo, hi = 0, self.n
        return R(self.ap[p0:p1, lo:hi], self.reg(lo, hi, p0, p1))

    def v(self, dims, idx, p0=0, p1=128):
        names = "abcd"[:len(dims)]
        pat = "p (%s) -> p %s" % (" ".join(names), " ".join(names))
        kw = {n_: d for n_, d in zip(names[:-1], dims[:-1])}
        ap = self.ap[p0:p1, 0:_prod(dims)].rearrange(pat, **kw)
        ap = ap[(slice(None),) + tuple(idx)]
        lo = hi = 0
        for i, (d, ix) in enumerate(zip(dims, idx)):
            st = _prod(dims[i + 1:])
            if isinstance(ix, int):
                a0, a1 = ix, ix + 1
            else:
                a0, a1, _ = ix.indices(d)
            lo += a0 * st
            hi += (a1 - 1) * st
        return R(ap, self.reg(lo, hi + 1, p0, p1))


def bc(r, shape, axis):
    return R(r.ap.unsqueeze(axis).broadcast_to(list(shape)), r.region)


class Arena:
    def __init__(self, nc, name, words):
        self.t = nc.alloc_sbuf_tensor(name, [128, words], F32)
        self.ap = self.t.ap()
        self.words = words
        self.ptr = 0
        self.name = name
        self.peak = 0

    def alloc(self, elems, dtype=F32):
        words = elems if dtype != BF16 else (elems + 1) // 2
        words = (words + 7) // 8 * 8
        assert self.ptr + words <= self.words, "arena overflow %s: %d + %d > %d" % (self.name, self.ptr, words, self.words)
        t = Tile("sb", self.ap, self.ptr, words, dtype)
        self.ptr += words
        self.peak = max(self.peak, self.ptr)
        return t

    def mark(self):
        return self.ptr

    def reset(self, m):
        self.ptr = m


PP = np.array([(hh * 4 + m) * 64 + j for m in range(4) for hh in range(2) for j in range(64)])
NCIN = 22 * 256
L_ = 2
TTOK = 512
C_FFN1, C_MIX, C_FFN2 = 0, 16, 32
C_MU, C_OMU = 48, 62
C_W0, C_A0, C_KK, C_KA, C_RK, C_LNW, C_LNB = 76, 80, 84, 88, 92, 96, 100
C_LB0, C_LB1, C_LB, C_OLB = 104, 108, 112, 116
C_GN = 120
C_DSK, C_GLB = 121, 125
C_LRE, C_LIM, C_LST = 129, 145, 161
C_QN, C_KN, C_QNS, C_KNS, C_INVF, C_SGN = 177, 178, 179, 180, 181, 182
C_TH, C_RM, C_KRE, C_NKRE, C_NKIM, C_KIM = 183, 199, 215, 231, 247, 263
NV = 280


def paired(vec512):
    return np.ascontiguousarray(vec512.reshape(2, 4, 64).transpose(0, 2, 1).reshape(128, 4))


def fm(vec, k):
    return np.ascontiguousarray(vec.reshape(k, 128).T)


def pack_layer(inp, l):
    f32 = np.float32
    w_in = inp["w_in"][l]
    A0, B0, C0, D0 = 0, 1792, 2560, 3072
    win = np.zeros((2048, NCIN), f32)
    col = 0

    def put(cols):
        nonlocal col
        n = cols.shape[1]
        win[:, col:col + n] = cols
        col += n
    for blk in range(3):
        put(w_in[:, A0 + blk * 512 + PP])
    put(w_in[:, A0 + 1536:A0 + 1664])
    put(w_in[:, A0 + 1664:A0 + 1792])
    for blk in range(4):
        put(w_in[:, D0 + blk * 512 + PP])
    put(w_in[:, C0:C0 + 512])
    put(w_in[:, B0 + 640:B0 + 768])
    col += 128
    assert col == 36 * 128
    put(w_in[:, B0:B0 + 640])
    sw = []
    for h in range(10):
        c0 = B0 + h * 64
        sw.append(w_in[:, c0 + 8:c0 + 16])
        sw.append(w_in[:, c0:c0 + 8])
    put(np.concatenate(sw, axis=1))
    pv = np.zeros((128, NV), f32)
    pv[:, C_FFN1:C_FFN1 + 16] = fm(inp["ffn1_norm"][l], 16)
    pv[:, C_MIX:C_MIX + 16] = fm(inp["mix_norm"][l], 16)
    pv[:, C_FFN2:C_FFN2 + 16] = fm(inp["ffn2_norm"][l], 16)
    mu = inp["rwkv_mu"][l]
    for blk in range(3):
        pv[:, C_MU + blk * 4:C_MU + blk * 4 + 4] = paired(mu[blk * 512:(blk + 1) * 512])
    pv[:, C_MU + 12] = mu[1536:1664]
    pv[:, C_MU + 13] = mu[1664:1792]
    for c, nm in ((C_W0, "rwkv_w0"), (C_A0, "rwkv_a0"), (C_KK, "rwkv_k_k"), (C_KA, "rwkv_k_a"),
                  (C_LNW, "rwkv_ln_w"), (C_LNB, "rwkv_ln_b")):
        pv[:, c:c + 4] = paired(inp[nm][l])
    pv[:, C_RK:C_RK + 4] = paired(inp["rwkv_r_k"][l].reshape(512))
    pv[:, C_LB0:C_LB0 + 4] = paired(inp["hgrn_lower_bounds"][0])
    pv[:, C_LB1:C_LB1 + 4] = paired(inp["hgrn_lower_bounds"][1])
    pv[:, C_GN] = np.tile(inp["hgrn_g_norm"][l], 2)
    pv[:, C_DSK:C_DSK + 4] = fm(inp["s5_d"][l], 4)
    pv[:, C_GLB:C_GLB + 4] = fm(inp["s5_glu_b"][l], 4)
    pv[:, C_LRE:C_LRE + 16] = fm(inp["s5_lambda_re"][l].reshape(2048), 16)
    pv[:, C_LIM:C_LIM + 16] = fm(inp["s5_lambda_im"][l].reshape(2048), 16)
    pv[:, C_LST:C_LST + 16] = fm(np.repeat(inp["s5_log_step"][l], 64), 16)
    qn, kn = inp["attn_q_norm"][l], inp["attn_k_norm"][l]
    pv[:64, C_QN] = qn
    pv[:64, C_KN] = kn
    pv[:16, C_QNS] = np.concatenate([qn[8:16], qn[0:8]])
    pv[:16, C_KNS] = np.concatenate([kn[8:16], kn[0:8]])
    invf = (500000.0 ** (-np.arange(0, 16, 2, dtype=np.float32) / 16.0)).astype(f32)
    pv[:16, C_INVF] = np.concatenate([invf, invf])
    pv[:16, C_SGN] = np.concatenate([-np.ones(8, f32), np.ones(8, f32)])
    waup = np.concatenate([inp["rwkv_w_up"][l][:, PP], inp["rwkv_a_up"][l][:, PP]], axis=0)
    gup = inp["rwkv_g_up"][l][:, PP]
    bre, bim = inp["s5_b_re"][l], inp["s5_b_im"][l]
    cre, cim = inp["s5_c_re"][l], inp["s5_c_im"][l]
    s5b = np.zeros((128, 2, 16, 128), f32)
    s5c = np.zeros((128, 2, 16, 128), f32)
    for g in range(32):
        st, gl, g8 = g // 2, g % 2, g % 8
        s5b[g8 * 16:(g8 + 1) * 16, 0, st, gl * 64:(gl + 1) * 64] = bre[g].T
        s5b[g8 * 16:(g8 + 1) * 16, 1, st, gl * 64:(gl + 1) * 64] = bim[g].T
        s5c[gl * 64:(gl + 1) * 64, 0, st, g8 * 16:(g8 + 1) * 16] = cre[g].T
        s5c[gl * 64:(gl + 1) * 64, 1, st, g8 * 16:(g8 + 1) * 16] = cim[g].T
    wout = inp["w_out"][l]
    wout_p = np.concatenate([wout[0:512][PP], wout[512:1024], wout[1024:1536], wout[1536:2048][PP]], axis=0)
    sinks_b = np.tile(inp["attn_sinks"][l][None, :], (128, 1))
    return dict(win=win, pv=pv, waup=np.ascontiguousarray(waup), gup=np.ascontiguousarray(gup),
                s5b=s5b.reshape(128, 4096), s5c=s5c.reshape(128, 4096),
                gluw=np.ascontiguousarray(inp["s5_glu_w"][l]), wout=wout_p, sinks=sinks_b.astype(f32))


def pack_all(inp, layers=(0, 1)):
    per = [pack_layer(inp, l) for l in layers]
    out = {k: np.ascontiguousarray(np.stack([p[k] for p in per])) for k in per[0]}
    for nm in ("ffn1_w_gate", "ffn1_w_up", "ffn1_w_down", "ffn2_w_gate", "ffn2_w_up", "ffn2_w_down"):
        out[nm] = np.ascontiguousarray(np.stack([inp[nm][l] for l in layers]))
    return out


PI = float(np.pi)


class Emit:
    def __init__(self, nc, nlayers=2, TT=TTOK, arena_words=49000):
        self.nc = nc
        self.S = Sched(nc)
        self.TT = TT
        self.NL = nlayers
        self.A = Arena(nc, "arena", arena_words)
        A = self.A
        self.ps_t = nc.alloc_psum_tensor("psum", [128, 4096], F32)
        self.ps_ap = self.ps_t.ap()
        self.banks = [Tile("ps", self.ps_ap, i * 512, 512, F32) for i in range(8)]
        self.banks_bf = [Tile("ps", self.ps_ap, i * 512, 512, BF16) for i in range(8)]
        self.bank_rr = 0
        self.xT = A.alloc(KT * TT, F32)
        self.hT = A.alloc(KT * TT, BF16)
        self.ident = A.alloc(128, F32)
        self.ident_bf = A.alloc(128, BF16)
        self.ones_bf = A.alloc(128, BF16)
        self.bones = A.alloc(128, BF16)
        self.msk = A.alloc(3 * 64, F32)
        self.amask = A.alloc(2 * 128, BF16)
        self.cmask = A.alloc(4 * TT, BF16)
        self.iota1 = A.alloc(TT, F32)
        self.cst = A.alloc(8, F32)
        self.perm16 = A.alloc(16, F32)
        self.rstd = A.alloc(TT, F32)
        self.pv = A.alloc(nlayers * NV, F32)
        self.sinks = A.alloc(nlayers * 8, F32)
        self.st = []
        for l in range(nlayers):
            d = dict(rwS=A.alloc(512, F32), rwSb=A.alloc(512, BF16), rwC=A.alloc(16, F32),
                     hgS=A.alloc(512, F32), hgSb=A.alloc(512, BF16),
                     kprev=A.alloc(256, BF16), vaug=A.alloc(5 * 2 * 65 + 6, BF16), s5h=A.alloc(32, F32))
            self.st.append(d)
        self.base_mark = A.mark()

    def bank(self):
        i = self.bank_rr % 8
        self.bank_rr += 1
        return self.banks[i], self.banks_bf[i]

    def ck(self, name):
        import os
        ks = os.environ.get('KSTOP', '')
        nm, _, cnt = ks.partition(':')
        if nm == name:
            self.ckn = getattr(self, 'ckn', 0) + 1
            if self.ckn >= int(cnt or 1):
                self.stopped = True

    def op(self, eng, fn, reads, writes, **kw):
        if getattr(self, 'stopped', False) and not kw.get('force'):
            return None
        kw.pop('force', None)
        rg = lambda x: (x.region if isinstance(x, R) else x.reg())
        return self.S.add(eng, fn, [rg(x) for x in reads], [rg(x) for x in writes], **kw)

    def tt(self, o, a, b, op, eng="dve"):
        self.op(eng, lambda e: e.tensor_tensor(o.ap, a.ap, b.ap, op), [a, b], [o])

    def ts(self, o, a, s1, s2, op0, op1=None, eng="dve"):
        rd = [a] + [s for s in (s1, s2) if isinstance(s, R)]
        v1 = s1.ap if isinstance(s1, R) else s1
        v2 = s2.ap if isinstance(s2, R) else s2
        if op1 is None:
            self.op(eng, lambda e: e.tensor_scalar(o.ap, a.ap, v1, None, op0), rd, [o])
        else:
            self.op(eng, lambda e: e.tensor_scalar(o.ap, a.ap, v1, v2, op0, op1), rd, [o])

    def stt(self, o, a, s, b, op0, op1):
        rd = [a, b] + ([s] if isinstance(s, R) else [])
        sv = s.ap if isinstance(s, R) else s
        self.op("dve", lambda e: e.scalar_tensor_tensor(o.ap, a.ap, sv, b.ap, op0, op1), rd, [o])

    def act(self, o, a, func, bias=None, scale=None):
        rd = [a] + ([bias] if isinstance(bias, R) else [])
        kw = {}
        if bias is not None:
            kw["bias"] = bias.ap if isinstance(bias, R) else bias
        if scale is not None:
            if isinstance(scale, R):
                rd.append(scale)
                kw["scale"] = scale.ap
            else:
                kw["scale"] = scale
        self.op("act", lambda e: e.activation(o.ap, a.ap, func, **kw), rd, [o])

    def cp(self, o, a, eng="dve"):
        if eng == "act":
            self.op("act", lambda e: e.copy(o.ap, a.ap), [a], [o])
        else:
            self.op(eng, lambda e: e.tensor_copy(o.ap, a.ap), [a], [o])

    def mm(self, o, l, r, start=True, stop=True):
        self.op("pe", lambda e: e.matmul(o.ap, l.ap, r.ap, start=start, stop=stop), [l, r], [o])

    def tr(self, o, a, ident):
        self.op("pe", lambda e: e.transpose(o.ap, a.ap, ident.ap), [a, ident], [o])

    def dma(self, eng, o, a, semkey, reads=(), writes=(), **kw):
        self.op(eng, lambda e: e.dma_start(out=o, in_=a, **kw), list(reads), list(writes), dma=True, semkey=semkey)

    def memset(self, o, val, eng="pool"):
        self.op(eng, lambda e: e.memset(o.ap, val), [], [o])

    def rsqrt_act(self, o, a, scale, eps_col):
        self.act(o, a, AF.Ln, bias=eps_col, scale=scale)
        self.act(o, o, AF.Exp, scale=-0.5)

    def pvc(self, l, c, n=1, p0=0, p1=128):
        return self.pv.r(l * NV + c, l * NV + c + n, p0, p1)

    def setup_common(self, dr):
        S = self.S
        self.dr = dr
        self.first = True
        ident, ones = self.ident, self.ones_bf
        self.memset(ident.r(), 0.0)
        self.op("pool", lambda e: e.affine_select(ident.ap, ident.ap, [[-1, 128]], ALU.not_equal, 1.0,
                                                  base=0, channel_multiplier=1), [ident.r()], [ident.r()])
        self.cp(self.ident_bf.r(), ident.r(), eng="pool")
        self.cp(self.perm16.r(0, 8, 0, 16), ident.r(8, 16, 0, 16), eng="pool")
        self.cp(self.perm16.r(8, 16, 0, 16), ident.r(0, 8, 0, 16), eng="pool")
        self.memset(ones.r(), 1.0)
        self.memset(self.bones.r(), 1.0)
        self.memset(self.bones.r(64, 128, 0, 64), 0.0)
        self.memset(self.bones.r(0, 64, 64, 128), 0.0)
        msk = self.msk
        self.memset(msk.r(), 1.0)
        for i, (pat, cm, cmp_) in enumerate((([[1, 64]], -1, ALU.is_gt), ([[1, 64]], -1, ALU.is_ge), ([[-1, 64]], 1, ALU.is_gt))):
            v = msk.r(i * 64, (i + 1) * 64, 0, 64)
            self.op("pool", lambda e, v=v, pat=pat, cm=cm, cmp_=cmp_: e.affine_select(
                v.ap, v.ap, pat, cmp_, 0.0, base=0, channel_multiplier=cm), [v], [v])
        am = self.amask
        self.memset(am.r(), 1.0)
        for i, (pat, cm, cmp_) in enumerate((([[1, 128]], -1, ALU.is_ge), ([[-1, 128]], 1, ALU.is_gt))):
            v = am.r(i * 128, (i + 1) * 128)
            self.op("pool", lambda e, v=v, pat=pat, cm=cm, cmp_=cmp_: e.affine_select(
                v.ap, v.ap, pat, cmp_, 0.0, base=0, channel_multiplier=cm), [v], [v])
        self.memset(self.cmask.r(), 1.0)
        self.memset(self.cmask.v((4 * self.TT // 64, 64), (slice(None), 0)), 0.0)
        io = self.iota1
        self.op("pool", lambda e: e.iota(io.ap, [[1, self.TT]], base=1, channel_multiplier=0,
                                         allow_small_or_imprecise_dtypes=True), [], [io.r()])
        for i, val in enumerate((1e-6, 64e-5, 0.0, PI / 2)):
            self.memset(self.cst.r(i, i + 1), val)
        for l in range(self.NL):
            self.dma("sp", self.pv.ap[:, l * NV:(l + 1) * NV], dr["pv"][l], ("pv", l),
                     writes=[self.pv.r(l * NV, (l + 1) * NV)])
            self.dma("sp", self.sinks.ap[:, l * 8:(l + 1) * 8], dr["sinks"][l], ("sk", l),
                     writes=[self.sinks.r(l * 8, (l + 1) * 8)])
        for l in range(self.NL):
            self.layer_consts(l)

    def sincos(self, osin, ocos, ang, tmp, ki):
        PL = 3.1415925
        self.ts(ki, ang, 1.0 / (2.0 * PI), None, ALU.mult)
        self.stt(tmp, ki, -2.0 * PI, ang, ALU.mult, ALU.add)
        self.ts(tmp, tmp, PL, -PL, ALU.min, ALU.max)
        self.act(osin, tmp, AF.Sin)
        self.stt(tmp, tmp, -1.0, tmp, ALU.mult, ALU.max)
        p0, p1 = tmp.region[1], tmp.region[2]
        self.act(ocos, tmp, AF.Sin, bias=self.cst.r(3, 4, p0, p1), scale=-1.0)

    def layer_consts(self, l):
        A = self.A
        m = A.mark()
        pc = lambda c, n=1, p0=0, p1=128: self.pvc(l, c, n, p0, p1)
        self.ts(pc(C_OMU, 14), pc(C_MU, 14), -1.0, 1.0, ALU.mult, ALU.add)
        if l == 0:
            self.memset(pc(C_LB, 4), 0.0)
        else:
            t = A.alloc(4, F32)
            self.tt(t.r(0, 4), pc(C_LB1, 4), pc(C_LB0, 4), ALU.subtract)
            self.act(pc(C_LB, 4), t.r(0, 4), AF.Sigmoid)
        self.ts(pc(C_OLB, 4), pc(C_LB, 4), -1.0, 1.0, ALU.mult, ALU.add)
        sk = self.sinks.r(l * 8, (l + 1) * 8)
        self.act(sk, sk, AF.Exp)
        t = [A.alloc(16, F32) for _ in range(8)]
        dt, sn, cs_, tmp, aa, bb, den, t2 = [x.r() for x in t]
        self.act(dt, pc(C_LST, 16), AF.Exp)
        self.tt(tmp, pc(C_LRE, 16), dt, ALU.mult)
        self.act(pc(C_RM, 16), tmp, AF.Exp)
        self.tt(pc(C_TH, 16), pc(C_LIM, 16), dt, ALU.mult)
        ki16 = A.alloc(16, I32).r()
        self.sincos(sn, cs_, pc(C_TH, 16), tmp, ki16)
        self.tt(aa, pc(C_RM, 16), cs_, ALU.mult)
        self.ts(aa, aa, -1.0, None, ALU.add)
        self.tt(bb, pc(C_RM, 16), sn, ALU.mult)
        self.tt(den, pc(C_LRE, 16), pc(C_LRE, 16), ALU.mult)
        self.tt(tmp, pc(C_LIM, 16), pc(C_LIM, 16), ALU.mult)
        self.tt(den, den, tmp, ALU.add)
        self.op("dve", lambda e: e.reciprocal(den.ap, den.ap), [den], [den])
        self.tt(tmp, aa, pc(C_LRE, 16), ALU.mult)
        self.tt(t2, bb, pc(C_LIM, 16), ALU.mult)
        self.tt(tmp, tmp, t2, ALU.add)
        self.tt(pc(C_KRE, 16), tmp, den, ALU.mult)
        self.tt(tmp, bb, pc(C_LRE, 16), ALU.mult)
        self.tt(t2, aa, pc(C_LIM, 16), ALU.mult)
        self.tt(tmp, tmp, t2, ALU.subtract)
        self.tt(pc(C_KIM, 16), tmp, den, ALU.mult)
        self.ts(pc(C_NKRE, 16), pc(C_KRE, 16), -1.0, None, ALU.mult)
        self.ts(pc(C_NKIM, 16), pc(C_KIM, 16), -1.0, None, ALU.mult)
        A.reset(m)

    def make_scratch(self, L):
        nc = self.nc
        self.sc = {}
        for l in range(L):
            for f in ("ffn1", "ffn2"):
                self.sc[(f + "_g", l)] = nc.dram_tensor("sc_%s_g%d" % (f, l), [22, 128, 4096], BF16).ap()
                self.sc[(f + "_u", l)] = nc.dram_tensor("sc_%s_u%d" % (f, l), [22, 128, 4096], BF16).ap()
                self.sc[(f + "_d", l)] = nc.dram_tensor("sc_%s_d%d" % (f, l), [16, 128, FT * 128], BF16).ap()
            self.sc[("win", l)] = nc.dram_tensor("sc_win%d" % l, [22, 128, 4096], BF16).ap()
            self.sc[("wout", l)] = nc.dram_tensor("sc_wout%d" % l, [16, 128, 2048], BF16).ap()

    def dreg(self, key, i):
        return R(None, ("dr_%s_%d" % key, 0, 1, i, i + 1))

    def load_x_tile(self, x, it):
        TT, A = self.TT, self.A
        m = A.mark()
        nsub = TT // 128
        xin = A.alloc(nsub * D, F32)
        src = x[it * TT:(it + 1) * TT, :].rearrange("(s p) d -> p s d", p=128)
        xin3 = xin.ap.rearrange("p (s d) -> p s d", s=nsub)
        for s in range(nsub):
            self.dma("sp", xin3[:, s, :], src[:, s, :], ("xin", s), writes=[xin.r(s * D, (s + 1) * D)])
        for kt in range(KT):
            b, _ = self.bank()
            for s in range(nsub):
                self.tr(b.r(s * 128, (s + 1) * 128), xin.r(s * D + kt * 128, s * D + (kt + 1) * 128), self.ident)
            self.cp(self.xT.r(kt * TT, (kt + 1) * TT), b.r(0, TT), eng=("act" if kt % 2 == 0 else "dve"))
        A.reset(m)

    def store_x_tile(self, y, it):
        TT, A = self.TT, self.A
        m = A.mark()
        nsub = TT // 128
        xo = A.alloc(nsub * D, F32)
        xo3 = xo.ap.rearrange("p (s d) -> p s d", s=nsub)
        dst = y[it * TT:(it + 1) * TT, :].rearrange("(s p) d -> p s d", p=128)
        for s in range(nsub):
            for kq in range(KT // 4):
                b, _ = self.bank()
                for j in range(4):
                    kt = kq * 4 + j
                    self.tr(b.r(j * 128, (j + 1) * 128), self.xT.r(kt * TT + s * 128, kt * TT + (s + 1) * 128), self.ident)
                lo = s * D + kq * 512
                self.cp(xo.r(lo, lo + 512), b.r(0, 512), eng=("act" if kq % 2 == 0 else "dve"))
            self.dma("sp", dst[:, s, :], xo3[:, s, :], ("xout", s), reads=[xo.r(s * D, (s + 1) * D)])
        A.reset(m)

    def rmsnorm_fm(self, l, gcol):
        TT, A = self.TT, self.A
        m = A.mark()
        sq = A.alloc(KT * TT, BF16)
        b, _ = self.bank()
        for kt in range(KT):
            self.act(sq.r(kt * TT, (kt + 1) * TT), self.xT.r(kt * TT, (kt + 1) * TT), AF.Square)
            self.mm(b.r(0, TT), self.ones_bf.r(), sq.r(kt * TT, (kt + 1) * TT), start=(kt == 0), stop=(kt == KT - 1))
        rstd = self.rstd.r()
        self.rsqrt_act(rstd, b.r(0, TT), 1.0 / D, self.cst.r(0, 1))
        tmps = [A.alloc(TT, F32) for _ in range(2)]
        for kt in range(KT):
            t_ = tmps[kt % 2].r()
            self.tt(t_, self.xT.r(kt * TT, (kt + 1) * TT), rstd, ALU.mult)
            self.act(self.hT.r(kt * TT, (kt + 1) * TT), t_, AF.Identity, scale=self.pvc(l, gcol + kt))
        A.reset(m)

    def ffn(self, l, gcol, part, tag):
        TT, A = self.TT, self.A
        self.rmsnorm_fm(l, gcol)
        m = A.mark()
        HT = A.alloc(FT * TT, BF16)
        FC = 256
        nfc = DFF // FC
        wbuf = [[A.alloc(KT * FC, BF16) for _ in range(2)] for _ in range(2)]
        sg = [A.alloc(TT, F32) for _ in range(2)]
        for fc in range(nfc):
            slot = fc % 2
            for mi in range(2):
                wb = wbuf[mi][slot]
                key = (part + ("_g", "_u")[mi], l)
                if self.first:
                    src = self.dr[part + ("_w_gate", "_w_up")[mi]][l].rearrange("(k p) f -> p k f", p=128)
                    self.dma("pool", wb.ap.rearrange("p (k f) -> p k f", k=KT), src[:, :, fc * FC:(fc + 1) * FC],
                             ("w", tag, mi, slot), writes=[wb.r()])
                    self.dma("sp", self.sc[key][fc], wb.ap, ("ws", tag, mi, slot), reads=[wb.r()], writes=[self.dreg(key, fc)])
                else:
                    self.dma("sp", wb.ap, self.sc[key][fc], ("w", tag, mi, slot), reads=[self.dreg(key, fc)], writes=[wb.r()])
            for fi in range(FC // 128):
                ft = fc * (FC // 128) + fi
                (bg, _), (bu, _) = self.bank(), self.bank()
                for (mi, bb) in ((0, bg), (1, bu)):
                    for kt in range(KT):
                        self.mm(bb.r(0, TT), wbuf[mi][slot].v((KT, FC), (kt, slice(fi * 128, (fi + 1) * 128))),
                                self.hT.r(kt * TT, (kt + 1) * TT), start=(kt == 0), stop=(kt == KT - 1))
                sgt = sg[ft % 2]
                self.act(sgt.r(), bg.r(0, TT), AF.Silu)
                self.tt(HT.r(ft * TT, (ft + 1) * TT), sgt.r(), bu.r(0, TT), ALU.mult)
        DC2, FH = 256, FT // 2
        wdb = [A.alloc(FH * DC2, BF16) for _ in range(2)]
        key = (part + "_d", l)
        for dc2 in range(D // DC2):
            bos = [self.bank()[0], self.bank()[0]]
            for fh in range(2):
                idx = dc2 * 2 + fh
                wb = wdb[idx % 2]
                if self.first:
                    src = self.dr[part + "_w_down"][l].rearrange("(f p) d -> p f d", p=128)
                    self.dma("pool", wb.ap.rearrange("p (f d) -> p f d", f=FH), src[:, fh * FH:(fh + 1) * FH, dc2 * DC2:(dc2 + 1) * DC2],
                             ("wd", tag, idx % 2), writes=[wb.r()])
                    self.dma("sp", self.sc[key][idx], wb.ap, ("wds", tag, idx % 2), reads=[wb.r()], writes=[self.dreg(key, idx)])
                else:
                    self.dma("sp", wb.ap, self.sc[key][idx], ("wd", tag, idx % 2), reads=[self.dreg(key, idx)], writes=[wb.r()])
                for di in range(2):
                    for f in range(FH):
                        ft = fh * FH + f
                        self.mm(bos[di].r(0, TT), wb.v((FH, DC2), (f, slice(di * 128, (di + 1) * 128))),
                                HT.r(ft * TT, (ft + 1) * TT), start=(ft == 0), stop=(ft == FT - 1))
            for di in range(2):
                dt_ = dc2 * 2 + di
                xs = self.xT.r(dt_ * TT, (dt_ + 1) * TT)
                hs = sg[di].r()
                self.act(hs, bos[di].r(0, TT), AF.Identity, scale=0.5)
                self.tt(xs, xs, hs, ALU.add)
        A.reset(m)

    def proj(self, l, chunks, jobs, tag):
        TT, A = self.TT, self.A
        m = A.mark()
        wb = [A.alloc(KT * 256, BF16) for _ in range(2)]
        for i, ch in enumerate(chunks):
            w = wb[i % 2]
            if self.first:
                src = self.dr["win"][l].rearrange("(k p) c -> p k c", p=128)
                self.dma("pool", w.ap.rearrange("p (k c) -> p k c", k=KT), src[:, :, ch * 256:(ch + 1) * 256],
                         ("win", tag, i % 2), writes=[w.r()])
                self.dma("sp", self.sc[("win", l)][ch], w.ap, ("wins", tag, i % 2), reads=[w.r()], writes=[self.dreg(("win", l), ch)])
            else:
                self.dma("sp", w.ap, self.sc[("win", l)][ch], ("win", tag, i % 2), reads=[self.dreg(("win", l), ch)], writes=[w.r()])
            for (c0, M, cb, tok) in jobs[ch]:
                if not tok:
                    b, _ = self.bank()
                    for kt in range(KT):
                        self.mm(b.r(0, TT, 0, M), w.v((KT, 256), (kt, slice(c0, c0 + M))),
                                self.hT.r(kt * TT, (kt + 1) * TT), start=(kt == 0), stop=(kt == KT - 1))
                    cb(b)
                else:
                    for s in range(TT // 128):
                        b, _ = self.bank()
                        for kt in range(KT):
                            self.mm(b.r(0, M), self.hT.r(kt * TT + s * 128, kt * TT + (s + 1) * 128),
                                    w.v((KT, 256), (kt, slice(c0, c0 + M))), start=(kt == 0), stop=(kt == KT - 1))
                        cb(b, s)
        A.reset(m)

    def wout_part(self, l, mi, yT, tag):
        TT, A = self.TT, self.A
        m = A.mark()
        wb = [A.alloc(4 * 512, BF16) for _ in range(2)]
        for dq in range(4):
            w = wb[dq % 2]
            if self.first:
                src = self.dr["wout"][l][mi * 512:(mi + 1) * 512, :].rearrange("(k p) d -> p k d", p=128)
                self.dma("pool", w.ap.rearrange("p (k d) -> p k d", k=4), src[:, :, dq * 512:(dq + 1) * 512],
                         ("wo", tag, dq % 2), writes=[w.r()])
                self.dma("sp", self.sc[("wout", l)][mi * 4 + dq], w.ap, ("wos", tag, dq % 2), reads=[w.r()],
                         writes=[self.dreg(("wout", l), mi * 4 + dq)])
            else:
                self.dma("sp", w.ap, self.sc[("wout", l)][mi * 4 + dq], ("wo", tag, dq % 2),
                         reads=[self.dreg(("wout", l), mi * 4 + dq)], writes=[w.r()])
            for di in range(4):
                dt_ = dq * 4 + di
                b, _ = self.bank()
                for kt in range(4):
                    self.mm(b.r(0, TT), w.v((4, 512), (kt, slice(di * 128, (di + 1) * 128))),
                            yT.r(kt * TT, (kt + 1) * TT), start=(kt == 0), stop=(kt == 3))
                xs = self.xT.r(dt_ * TT, (dt_ + 1) * TT)
                self.tt(xs, b.r(0, TT), xs, ALU.add)
        A.reset(m)

    def chunk_engine(self, opd, S, Sb, gL, oT, dplr):
        A = self.A
        m0 = A.mark()
        L = 64
        nch = self.TT // L
        nop = 4 if dplr else 2
        NB = 4
        outs, ints = [], []
        for i in range(NB):
            d = dict(tok=A.alloc(4 * nop * 128, BF16), ArkT=A.alloc(512, BF16))
            if dplr:
                d.update(ArbT=A.alloc(512, BF16), Gb=A.alloc(512, BF16), WaT=A.alloc(256, BF16), M1=A.alloc(512, BF16))
            outs.append(d)
        for i in range(2):
            ints.append(dict(AakT=A.alloc(512, BF16), PQ=[A.alloc(512, BF16) for _ in range(4)], G=A.alloc(512, F32)) if dplr else {})
        U = A.alloc(4 * 128, BF16) if dplr else None
        su = bc(self.msk.r(0, 64, 0, 64), (64, 4, 64), 1)
        iu = bc(self.msk.r(64, 128, 0, 64), (64, 4, 64), 1)
        sl = bc(self.msk.r(128, 192, 0, 64), (64, 4, 64), 1)
        id64 = bc(self.ident.r(0, 64, 0, 64), (64, 8, 64), 1)
        names = ["v", "kS"] + (["aI", "bS"] if dplr else [])
        h8 = lambda t, h: t.v((8, 64), (h, slice(None)), 0, 64)
        all8 = lambda t: t.v((8, 64), (slice(None), slice(None)), 0, 64)
        pair = lambda t, m: t.v((2, 4, 64), (slice(None), m, slice(None)), 0, 64)

        def phaseA(c, o, it):
            cs_ = slice(c * L, (c + 1) * L)
            tk = o["tok"]
            tkv = lambda m, q, lo=0, hi=128: tk.v((4, nop, 128), (m, q, slice(lo, hi)), 0, 64)
            for mp in range(2):
                _, bb = self.bank()
                for mm_ in range(2):
                    m = mp * 2 + mm_
                    for q, nm in enumerate(names):
                        self.tr(bb.v((2, nop, 128), (mm_, q, slice(None)), 0, 64),
                                opd[nm].v((4, 512), (m, cs_)), self.ident_bf)
                self.cp(tk.v((4, nop, 128), (slice(mp * 2, mp * 2 + 2), slice(None), slice(None)), 0, 64),
                        bb.v((2, nop, 128), (slice(None), slice(None), slice(None)), 0, 64),
                        eng=("act" if mp == 0 else "dve"))
            yield
            def amat(lname, rname, dst, msk_):
                for hh in range(2):
                    b, _ = self.bank()
                    p0 = hh * 64
                    for m in range(4):
                        self.mm(b.v((4, 64), (m, slice(None)), 0, 64), opd[lname].v((4, 512), (m, cs_), p0, p0 + 64),
                                opd[rname].v((4, 512), (m, cs_), p0, p0 + 64))
                    self.tt(dst.v((8, 64), (slice(hh * 4, hh * 4 + 4), slice(None)), 0, 64),
                            b.v((4, 64), (slice(None), slice(None)), 0, 64), msk_, ALU.mult)
            amat("kI", "rI", o["ArkT"], iu)
            yield
            if not dplr:
                return
            AakT, G, Gb, WaT, M1 = it["AakT"], it["G"], o["Gb"], o["WaT"], o["M1"]
            P, Q, Pn, Qn = it["PQ"]
            amat("bI", "rI", o["ArbT"], iu)
            yield
            amat("kI", "aI", AakT, su)
            yield
            amat("bI", "aI", P, su)
            yield
            amat("aI", "bI", Q, sl)
            self.tt(all8(G), all8(P), id64, ALU.add, eng="pool")
            self.cp(all8(Gb), all8(G), eng="pool")
            yield
            for lev in range(5):
                last = lev == 4
                bq, _ = self.bank()
                if not last:
                    bp, _ = self.bank()
                for h in range(8):
                    if not last:
                        self.mm(h8(bp, h), h8(Q, h), h8(P, h))
                    self.mm(h8(bq, h), h8(P, h), h8(Q, h))
                if not last:
                    self.cp(all8(Pn), all8(bp), eng="act")
                self.cp(all8(Qn), all8(bq), eng="dve")
                yield
                bg, _ = self.bank()
                for h in range(8):
                    self.mm(h8(bg, h), h8(Qn, h), h8(Gb, h))
                self.tt(all8(G), all8(bg), all8(G), ALU.add)
                self.cp(all8(Gb), all8(G), eng=("act" if lev % 2 else "pool"))
                P, Q, Pn, Qn = Pn, Qn, P, Q
                yield
            bw, _ = self.bank()
            for m in range(4):
                self.mm(bw.v((4, 128), (m, slice(None))), tkv(m, 2), pair(Gb, m))
            for hh in range(2):
                p0 = hh * 64
                self.cp(WaT.v((4, 64), (slice(None), slice(None)), p0, p0 + 64),
                        bw.v((4, 128), (slice(None), slice(p0, p0 + 64)), p0, p0 + 64),
                        eng=("act" if hh == 0 else "dve"))
            yield
            bm, _ = self.bank()
            for h in range(8):
                hh, m = h // 4, h % 4
                self.mm(h8(bm, h), h8(AakT, h), tkv(m, 0, hh * 64, hh * 64 + 64))
            self.cp(all8(M1), all8(bm), eng="act")
            yield

        def phaseB(c, o):
            cs_ = slice(c * L, (c + 1) * L)
            tk = o["tok"]
            tkv = lambda m, q, lo=0, hi=128: tk.v((4, nop, 128), (m, q, slice(lo, hi)), 0, 64)
            if dplr:
                Gb, WaT, M1 = o["Gb"], o["WaT"], o["M1"]
                bu, _ = self.bank()
                for m in range(4):
                    self.mm(bu.v((4, 128), (m, slice(None)), 0, 64), WaT.v((4, 64), (m, slice(None))),
                            Sb.v((4, 128), (m, slice(None))), start=True, stop=False)
                    for hh in range(2):
                        h = hh * 4 + m
                        self.mm(bu.v((4, 128), (m, slice(hh * 64, hh * 64 + 64)), 0, 64), h8(Gb, h), h8(M1, h),
                                start=False, stop=True)
                self.cp(U.v((4, 128), (slice(None), slice(None)), 0, 64),
                        bu.v((4, 128), (slice(None), slice(None)), 0, 64), eng="dve")
                yield
            by, _ = self.bank()
            for m in range(4):
                rS = opd["rS"].v((4, 512), (m, cs_))
                self.mm(by.v((4, 128), (m, slice(None))), Sb.v((4, 128), (m, slice(None))),
                        bc(rS, (128, 2, 64), 1), start=True, stop=False)
                self.mm(by.v((4, 128), (m, slice(None))), tkv(m, 0), pair(o["ArkT"], m), start=False, stop=not dplr)
                if dplr:
                    self.mm(by.v((4, 128), (m, slice(None))), U.v((4, 128), (m, slice(None)), 0, 64), pair(o["ArbT"], m),
                            start=False, stop=True)
            for hh in range(2):
                p0 = hh * 64
                self.cp(oT.v((4, 512), (slice(None), cs_), p0, p0 + 64),
                        by.v((4, 128), (slice(None), slice(p0, p0 + 64)), p0, p0 + 64),
                        eng=("act" if hh == 0 else "dve"))
            bs, _ = self.bank()
            for m in range(4):
                self.mm(bs.v((4, 128), (m, slice(None))), tkv(m, 1), tkv(m, 0), start=True, stop=not dplr)
                if dplr:
                    self.mm(bs.v((4, 128), (m, slice(None))), tkv(m, 3), U.v((4, 128), (m, slice(None)), 0, 64),
                            start=False, stop=True)
            for hh in range(2):
                p0 = hh * 64
                Sd = S.v((4, 128), (slice(None), slice(p0, p0 + 64)), p0, p0 + 64)
                g = bc(gL.v((4, 8), (slice(None), c), p0, p0 + 64), (64, 4, 64), 2)
                self.tt(Sd, Sd, g, ALU.mult, eng="pool")
                self.tt(Sd, bs.v((4, 128), (slice(None), slice(p0, p0 + 64)), p0, p0 + 64), Sd, ALU.add)
                self.cp(Sb.v((4, 128), (slice(None), slice(p0, p0 + 64)), p0, p0 + 64), Sd, eng="pool")
            yield

        def seqB(cs):
            for c in cs:
                yield from phaseB(c, outs[c % NB])

        def drive(gens):
            gens = list(gens)
            while gens:
                for g in list(gens):
                    try:
                        next(g)
                    except StopIteration:
                        gens.remove(g)
        prevB = None
        for grp in range(nch // 2):
            c0 = 2 * grp
            gens = [phaseA(c0, outs[c0 % NB], ints[0]), phaseA(c0 + 1, outs[(c0 + 1) % NB], ints[1])]
            if prevB is not None:
                gens.append(prevB)
            drive(gens)
            prevB = seqB([c0, c0 + 1])
        drive([prevB])
        A.reset(m0)

    def head_rstd(self, src_bf, out, eps_col, scale=1.0 / 64):
        b, _ = self.bank()
        self.mm(b.r(0, self.TT), self.bones.r(), src_bf)
        self.rsqrt_act(out, b.r(0, self.TT), scale, eps_col)

    def rwkv(self, l, dr, seq_start):
        TT, A = self.TT, self.A
        st = self.st[l]
        m0 = A.mark()
        pc = lambda c, n=1, p0=0, p1=128: self.pvc(l, c, n, p0, p1)
        ya = A.alloc(4 * TT, BF16)
        opn = ["rI", "aI", "bI", "kI", "kS", "bS", "v"]
        opd = {n: A.alloc(4 * TT, BF16) for n in opn}
        opd["rS"] = opd["rI"]
        bonus, g_bf = A.alloc(4 * TT, BF16), A.alloc(4 * TT, BF16)
        gL = A.alloc(32, F32)
        m1 = A.mark()
        bufs = [A.alloc(4 * (TT + 1), F32) for _ in range(3)]
        bufwa, bufg = A.alloc(TT + 1, F32), A.alloc(TT + 1, F32)
        waup, gup = A.alloc(512, BF16), A.alloc(512, BF16)
        self.dma("pool", waup.ap, dr["waup"][l], ("rwp", 0), writes=[waup.r()])
        self.dma("pool", gup.ap, dr["gup"][l], ("rwp", 1), writes=[gup.r()])
        if seq_start:
            self.memset(st["rwS"].r(), 0.0)
            self.memset(st["rwSb"].r(), 0.0)
            self.memset(st["rwC"].r(), 0.0)
        bv = lambda i, m, lo=1, hi=TT + 1: bufs[i].v((4, TT + 1), (m, slice(lo, hi)))
        for i in range(3):
            self.cp(bufs[i].v((4, TT + 1), (slice(None), 0)), st["rwC"].r(i * 4, i * 4 + 4), eng="pool")
        self.cp(bufwa.r(0, 1), st["rwC"].r(12, 13), eng="pool")
        self.cp(bufg.r(0, 1), st["rwC"].r(13, 14), eng="pool")
        jobs = {}
        for blk in range(14):
            if blk < 12:
                dst = bv(blk // 4, blk % 4)
            else:
                dst = (bufwa if blk == 12 else bufg).r(1, TT + 1)
            jobs.setdefault(blk // 2, []).append(((blk % 2) * 128, 128,
                                                  (lambda b, dst=dst, blk=blk: self.cp(dst, b.r(0, TT), eng=("act" if blk % 2 else "dve"))), False))
        self.proj(l, list(range(7)), jobs, "rw")
        self.ck("rw_proj")
        cs = A.alloc(4 * TT, F32)
        tmp = A.alloc(4 * TT, F32)
        a_bf, kkn, kp, b_bf = [A.alloc(4 * TT, BF16) for _ in range(4)]
        small = A.alloc(2 * TT, BF16)
        for i in range(3):
            self.cp(st["rwC"].r(i * 4, i * 4 + 4), bufs[i].v((4, TT + 1), (slice(None), TT)), eng="pool")
        self.cp(st["rwC"].r(12, 13), bufwa.r(TT, TT + 1), eng="pool")
        self.cp(st["rwC"].r(13, 14), bufg.r(TT, TT + 1), eng="pool")
        for blk in range(14):
            if blk < 12:
                cur, prv = bv(blk // 4, blk % 4), bv(blk // 4, blk % 4, 0, TT)
            else:
                bf_ = bufwa if blk == 12 else bufg
                cur, prv = bf_.r(1, TT + 1), bf_.r(0, TT)
            t_ = tmp.r((blk % 2) * 2 * TT, (blk % 2) * 2 * TT + TT)
            t2_ = tmp.r((blk % 2) * 2 * TT + TT, (blk % 2) * 2 * TT + 2 * TT)
            self.act(t_, prv, AF.Identity, scale=pc(C_MU + blk))
            self.act(t2_, cur, AF.Identity, scale=pc(C_OMU + blk))
            self.tt(cur, t_, t2_, ALU.add)
        self.ck('rw_lerp')
        r4 = bufs[0].v((4, TT + 1), (slice(None), slice(1, TT + 1)))
        v4 = bufs[2].v((4, TT + 1), (slice(None), slice(1, TT + 1)))
        f4 = lambda t: t.v((4, TT), (slice(None), slice(None)))
        low = small
        self.act(low.r(0, TT, 0, 64), bufwa.r(1, TT + 1, 0, 64), AF.Tanh)
        self.cp(low.r(0, TT, 64, 128), bufwa.r(1, TT + 1, 64, 128), eng="dve")
        for m in range(4):
            b, _ = self.bank()
            self.mm(b.r(0, TT), waup.r(m * 128, (m + 1) * 128, 0, 64), low.r(0, TT, 0, 64))
            self.act(tmp.r(m * TT, (m + 1) * TT), b.r(0, TT), AF.Sigmoid, bias=pc(C_W0 + m))
            b, _ = self.bank()
            self.mm(b.r(0, TT), waup.r(m * 128, (m + 1) * 128, 64, 128), low.r(0, TT, 64, 128))
            self.act(a_bf.r(m * TT, (m + 1) * TT), b.r(0, TT), AF.Sigmoid, bias=pc(C_A0 + m))
        self.ts(tmp.r(), tmp.r(), -0.6065306597126334, None, ALU.mult)
        self.op("dve", lambda e: e.tensor_tensor_scan(cs.ap, self.cmask.ap, tmp.ap, 0.0, ALU.mult, ALU.add),
                [self.cmask.r(), tmp.r()], [cs.r()])
        sgl = small.r(TT, 2 * TT)
        self.act(sgl, bufg.r(1, TT + 1), AF.Sigmoid)
        for m in range(4):
            b, _ = self.bank()
            self.mm(b.r(0, TT), gup.r(m * 128, (m + 1) * 128), sgl)
            self.cp(g_bf.r(m * TT, (m + 1) * TT), b.r(0, TT), eng="act")
        self.ck('rw_low')
        self.tt(tmp.r(), cs.r(), tmp.r(), ALU.subtract)
        self.act(tmp.r(), tmp.r(), AF.Exp)
        eprev = tmp
        for m in range(4):
            t_ = small.r(0, TT)
            kf = A.alloc(TT, F32)
            self.act(kf.r(), bv(1, m), AF.Identity, scale=pc(C_KK + m))
            self.act(t_, kf.r(), AF.Square)
            b, _ = self.bank()
            self.mm(b.r(0, TT), self.bones.r(), t_)
            rs = A.alloc(TT, F32)
            self.act(rs.r(), b.r(0, TT), AF.Ln)
            self.act(rs.r(), rs.r(), AF.Exp, scale=-0.5)
            self.tt(kkn.r(m * TT, (m + 1) * TT), kf.r(), rs.r(), ALU.mult)
            self.ts(kf.r(), a_bf.r(m * TT, (m + 1) * TT), 1.0, pc(C_KA + m), ALU.subtract, ALU.mult)
            self.stt(kp.r(m * TT, (m + 1) * TT), kf.r(), 1.0, bv(1, m), ALU.add, ALU.mult)
            A.reset(A.mark() - 2 * TT)
        self.tt(b_bf.r(), kkn.r(), a_bf.r(), ALU.mult, eng="pool")
        self.stt(opd["aI"].r(), kkn.r(), -1.0, eprev.r(), ALU.mult, ALU.mult)
        self.ck('rw_kk')
        for m in range(4):
            t_ = small.r(0, TT)
            self.stt(t_, bv(0, m), pc(C_RK + m), kp.r(m * TT, (m + 1) * TT), ALU.mult, ALU.mult)
            b, _ = self.bank()
            self.mm(b.r(0, TT), self.bones.r(), t_)
            self.tt(bonus.r(m * TT, (m + 1) * TT), b.r(0, TT), bv(2, m), ALU.mult)
        self.cp(f4(opd["v"]), v4, eng="pool")
        self.act(gL.r(), cs.v((4 * TT // 64, 64), (slice(None), 63)), AF.Exp)
        self.act(tmp.r(), cs.r(), AF.Exp)
        self.tt(f4(opd["rI"]), r4, f4(tmp), ALU.mult)
        self.act(tmp.r(), cs.r(), AF.Exp, scale=-1.0)
        self.tt(opd["bI"].r(), b_bf.r(), tmp.r(), ALU.mult)
        self.tt(opd["kI"].r(), kp.r(), tmp.r(), ALU.mult, eng="pool")
        nseg = 4 * TT // 64
        cL = bc(cs.v((nseg, 64), (slice(None), 63)), (128, nseg, 64), 2)
        self.tt(tmp.v((nseg, 64), (slice(None), slice(None))), cL, cs.v((nseg, 64), (slice(None), slice(None))), ALU.subtract)
        self.act(tmp.r(), tmp.r(), AF.Exp)
        self.tt(opd["kS"].r(), kp.r(), tmp.r(), ALU.mult)
        self.tt(opd["bS"].r(), b_bf.r(), tmp.r(), ALU.mult, eng="pool")
        A.reset(m1)
        oT = A.alloc(4 * TT, F32)
        self.ck("rw_ops")
        self.chunk_engine(opd, st["rwS"], st["rwSb"], gL, oT, True)
        self.ck("rw_ce")
        for m in range(4):
            sl_ = lambda t: t.r(m * TT, (m + 1) * TT)
            obf, cen, rs = A.alloc(TT, BF16), A.alloc(TT, F32), A.alloc(TT, F32)
            self.cp(obf.r(), sl_(oT), eng="act")
            b, _ = self.bank()
            self.mm(b.r(0, TT), self.bones.r(), obf.r())
            self.stt(cen.r(), b.r(0, TT), -1.0 / 64, sl_(oT), ALU.mult, ALU.add)
            self.act(obf.r(), cen.r(), AF.Square)
            self.head_rstd(obf.r(), rs.r(), self.cst.r(1, 2))
            self.tt(cen.r(), cen.r(), rs.r(), ALU.mult)
            self.ts(cen.r(), cen.r(), pc(C_LNW + m), pc(C_LNB + m), ALU.mult, ALU.add)
            self.tt(cen.r(), cen.r(), sl_(bonus), ALU.add)
            self.tt(sl_(ya), cen.r(), sl_(g_bf), ALU.mult)
            A.reset(A.mark() - (TT // 2 + 2 * TT))
        self.wout_part(l, 0, ya, "rw")
        A.reset(m0)

    def hgrn(self, l, dr, seq_start):
        TT, A = self.TT, self.A
        st = self.st[l]
        m0 = A.mark()
        pc = lambda c, n=1, p0=0, p1=128: self.pvc(l, c, n, p0, p1)
        yd = A.alloc(4 * TT, BF16)
        opd = {n: A.alloc(4 * TT, BF16) for n in ["rI", "kI", "rS", "kS", "v"]}
        sg_bf = A.alloc(4 * TT, BF16)
        gL = A.alloc(32, F32)
        m1 = A.mark()
        q_bf, k_bf = A.alloc(4 * TT, BF16), A.alloc(4 * TT, BF16)
        ld, cs, tmp = A.alloc(4 * TT, F32), A.alloc(4 * TT, F32), A.alloc(4 * TT, F32)
        if seq_start:
            self.memset(st["hgS"].r(), 0.0)
            self.memset(st["hgSb"].r(), 0.0)
        jobs = {}
        for blk in range(16):
            kind, m = blk // 4, blk % 4
            sl_ = lambda t, m=m: t.r(m * TT, (m + 1) * TT)
            if kind == 0:
                cb = lambda b, sl_=sl_: self.act(sl_(q_bf), b.r(0, TT), AF.Silu)
            elif kind == 1:
                def cb(b, sl_=sl_, m=m):
                    self.act(sl_(tmp), b.r(0, TT), AF.Sigmoid)
                    self.ts(sl_(tmp), sl_(tmp), pc(C_OLB + m), pc(C_LB + m), ALU.mult, ALU.add)
                    self.act(sl_(ld), sl_(tmp), AF.Ln)
                    self.ts(sl_(k_bf), sl_(tmp), -1.0, 1.0, ALU.mult, ALU.add)
            elif kind == 2:
                cb = lambda b, sl_=sl_: self.cp(sl_(opd["v"]), b.r(0, TT), eng="dve")
            else:
                cb = lambda b, sl_=sl_: self.act(sl_(sg_bf), b.r(0, TT), AF.Silu)
            jobs.setdefault(7 + blk // 2, []).append(((blk % 2) * 128, 128, cb, False))
        self.proj(l, list(range(7, 15)), jobs, "hg")
        self.ck("hg_proj")
        self.op("dve", lambda e: e.tensor_tensor_scan(cs.ap, self.cmask.ap, ld.ap, 0.0, ALU.mult, ALU.add),
                [self.cmask.r(), ld.r()], [cs.r()])
        self.ck('hg_scan')
        nseg = 4 * TT // 64
        seg = lambda t: t.v((nseg, 64), (slice(None), slice(None)))
        cm = bc(cs.v((nseg, 64), (slice(None), 31)), (128, nseg, 64), 2)
        cL = bc(cs.v((nseg, 64), (slice(None), 63)), (128, nseg, 64), 2)
        self.act(gL.r(), cs.v((nseg, 64), (slice(None), 63)), AF.Exp)
        self.tt(seg(ld), seg(cs), cm, ALU.subtract)
        self.act(tmp.r(), ld.r(), AF.Exp)
        self.stt(opd["rI"].r(), q_bf.r(), 0.125, tmp.r(), ALU.mult, ALU.mult)
        self.act(tmp.r(), ld.r(), AF.Exp, scale=-1.0)
        self.tt(opd["kI"].r(), k_bf.r(), tmp.r(), ALU.mult, eng="pool")
        self.act(tmp.r(), cs.r(), AF.Exp)
        self.stt(opd["rS"].r(), q_bf.r(), 0.125, tmp.r(), ALU.mult, ALU.mult)
        self.tt(seg(ld), cL, seg(cs), ALU.subtract)
        self.act(tmp.r(), ld.r(), AF.Exp)
        self.tt(opd["kS"].r(), k_bf.r(), tmp.r(), ALU.mult, eng="pool")
        A.reset(m1)
        oT = A.alloc(4 * TT, F32)
        self.ck("hg_ops")
        self.chunk_engine(opd, st["hgS"], st["hgSb"], gL, oT, False)
        self.ck("hg_ce")
        for m in range(4):
            sl_ = lambda t: t.r(m * TT, (m + 1) * TT)
            sq, rs = A.alloc(TT, BF16), A.alloc(TT, F32)
            self.act(sq.r(), sl_(oT), AF.Square)
            self.head_rstd(sq.r(), rs.r(), self.cst.r(0, 1))
            self.stt(rs.r(), sl_(oT), pc(C_GN), rs.r(), ALU.mult, ALU.mult)
            self.tt(sl_(yd), rs.r(), sl_(sg_bf), ALU.mult)
            A.reset(A.mark() - (TT // 2 + TT))
        self.wout_part(l, 3, yd, "hg")
        A.reset(m0)

    def s5(self, l, dr, seq_start):
        for _ in self.s5_gen(l, dr, seq_start):
            pass

    def s5_gen(self, l, dr, seq_start):
        TT, A = self.TT, self.A
        st = self.st[l]
        m0 = A.mark()
        pc = lambda c, n=1, p0=0, p1=128: self.pvc(l, c, n, p0, p1)
        yc = A.alloc(4 * TT, BF16)
        u32, ubf = A.alloc(4 * TT, F32), A.alloc(4 * TT, BF16)
        Bbd = A.alloc(4096, BF16)
        Cbd = A.alloc(4096, BF16)
        gluw = A.alloc(4 * 512, BF16)
        t1 = A.alloc(128, F32)
        mcf = A.mark()
        Cf = A.alloc(4096, F32)
        self.dma("pool", Bbd.ap, dr["s5b"][l], ("s5p", 0), writes=[Bbd.r()])
        self.dma("sp", Cf.ap, dr["s5c"][l], ("s5p", 1), writes=[Cf.r()])
        self.dma("pool", gluw.ap.rearrange("p (k c) -> p k c", k=4), dr["gluw"][l].rearrange("(k p) c -> p k c", p=128),
                 ("s5p", 2), writes=[gluw.r()])
        if seq_start:
            self.memset(st["s5h"].r(), 0.0)
        jobs = {}
        for blk in range(4):
            def cb(b, blk=blk):
                self.cp(u32.r(blk * TT, (blk + 1) * TT), b.r(0, TT), eng="act")
                self.cp(ubf.r(blk * TT, (blk + 1) * TT), b.r(0, TT), eng="dve")
            jobs.setdefault(15 + blk // 2, []).append(((blk % 2) * 128, 128, cb, False))
        self.proj(l, [15, 16], jobs, "s5")
        yield
        cv = lambda t, ri, s: t.v((2, 16, 128), (ri, s, slice(None)))
        for s in range(16):
            self.ts(t1.r(), cv(Cf, 0, s), pc(C_KRE + s), None, ALU.mult)
            self.stt(cv(Cbd, 0, s), cv(Cf, 1, s), pc(C_NKIM + s), t1.r(), ALU.mult, ALU.add)
            self.ts(t1.r(), cv(Cf, 0, s), pc(C_NKIM + s), None, ALU.mult)
            self.stt(cv(Cbd, 1, s), cv(Cf, 1, s), pc(C_NKRE + s), t1.r(), ALU.mult, ALU.add)
        yield
        A.reset(mcf)
        sets = []
        for _ in range(2):
            d = {n_: A.alloc(TT, F32) for n_ in ("sn", "cs", "ta", "tb", "tc", "td", "xre", "xim", "gre", "gim")}
            d["ang"] = d["ta"]
            d["kis"] = A.alloc(TT, I32)
            d["hre"], d["him"] = A.alloc(TT, BF16), A.alloc(TT, BF16)
            sets.append(d)
        yb = {}

        def st_gen(s):
            q4 = s // 4
            B_ = sets[s % 2]
            ang, sn, cs_, ta, tb, tc, td, xre, xim, gre, gim, kis, hre, him = [B_[n_] for n_ in (
                "ang", "sn", "cs", "ta", "tb", "tc", "td", "xre", "xim", "gre", "gim", "kis", "hre", "him")]
            if s % 4 == 0:
                yb[q4] = self.reserve()
            ybank = yb[q4]
            (bre, _), (bim, _) = self.bank(), self.bank()
            self.mm(bre.r(0, TT), cv(Bbd, 0, s), ubf.r(q4 * TT, (q4 + 1) * TT))
            self.mm(bim.r(0, TT), cv(Bbd, 1, s), ubf.r(q4 * TT, (q4 + 1) * TT))
            self.ts(ang.r(), self.iota1.r(), pc(C_TH + s), None, ALU.mult)
            self.sincos(sn.r(), cs_.r(), ang.r(), xre.r(), kis.r())
            yield
            self.tt(ta.r(), bre.r(0, TT), cs_.r(), ALU.mult)
            self.tt(tb.r(), bim.r(0, TT), sn.r(), ALU.mult)
            self.tt(tc.r(), bim.r(0, TT), cs_.r(), ALU.mult)
            self.tt(td.r(), bre.r(0, TT), sn.r(), ALU.mult)
            self.tt(xre.r(), ta.r(), tb.r(), ALU.add, eng="pool")
            self.tt(xim.r(), tc.r(), td.r(), ALU.subtract, eng="pool")
            yield
            rmb = R(pc(C_RM + s).ap.broadcast_to([128, TT]), pc(C_RM + s).region)
            hr0, hi0 = st["s5h"].r(s, s + 1), st["s5h"].r(16 + s, 17 + s)
            self.op("dve", lambda e, rmb=rmb, hr0=hr0, gre=gre, xre=xre: e.tensor_tensor_scan(gre.ap, rmb.ap, xre.ap, hr0.ap, ALU.mult, ALU.add),
                    [rmb, xre.r(), hr0], [gre.r()])
            self.op("dve", lambda e, rmb=rmb, hi0=hi0, gim=gim, xim=xim: e.tensor_tensor_scan(gim.ap, rmb.ap, xim.ap, hi0.ap, ALU.mult, ALU.add),
                    [rmb, xim.r(), hi0], [gim.r()])
            yield
            self.tt(ta.r(), gre.r(), cs_.r(), ALU.mult, eng="pool")
            self.tt(tb.r(), gim.r(), sn.r(), ALU.mult, eng="pool")
            self.tt(tc.r(), gim.r(), cs_.r(), ALU.mult, eng="pool")
            self.tt(td.r(), gre.r(), sn.r(), ALU.mult, eng="pool")
            yield
            self.tt(hre.r(), ta.r(), tb.r(), ALU.subtract)
            self.tt(him.r(), tc.r(), td.r(), ALU.add)
            c5, s5_ = cs_.r(TT - 1, TT), sn.r(TT - 1, TT)
            t2 = t1.r(s % 2, s % 2 + 1)
            t3 = t1.r(2 + s % 2, 3 + s % 2)
            self.ts(t2, gim.r(TT - 1, TT), s5_, None, ALU.mult)
            self.ts(t3, gre.r(TT - 1, TT), s5_, None, ALU.mult)
            self.stt(hr0, gre.r(TT - 1, TT), c5, t2, ALU.mult, ALU.subtract)
            self.stt(hi0, gim.r(TT - 1, TT), c5, t3, ALU.mult, ALU.add)
            self.mm(ybank.r(0, TT), cv(Cbd, 0, s), hre.r(), start=(s % 4 == 0), stop=False)
            self.mm(ybank.r(0, TT), cv(Cbd, 1, s), him.r(), start=False, stop=(s % 4 == 3))
            if s % 4 == 3:
                us = u32.r(q4 * TT, (q4 + 1) * TT)
                self.stt(us, us, pc(C_DSK + q4), ybank.r(0, TT), ALU.mult, ALU.add)
                self.act(us, us, AF.Gelu)
                self.cp(ubf.r(q4 * TT, (q4 + 1) * TT), us, eng="pool")
                self.release(ybank)

        for s0 in range(0, 16, 2):
            gens = [st_gen(s0), st_gen(s0 + 1)]
            while gens:
                for g_ in list(gens):
                    try:
                        next(g_)
                    except StopIteration:
                        gens.remove(g_)
                yield
        for q in range(4):
            b, _ = self.bank()
            for kt in range(4):
                self.mm(b.r(0, TT), gluw.v((4, 512), (kt, slice(q * 128, (q + 1) * 128))), ubf.r(kt * TT, (kt + 1) * TT),
                        start=(kt == 0), stop=(kt == 3))
            tq = sets[q % 2]["ta"]
            self.act(tq.r(), b.r(0, TT), AF.Sigmoid, bias=pc(C_GLB + q))
            self.tt(yc.r(q * TT, (q + 1) * TT), u32.r(q * TT, (q + 1) * TT), tq.r(), ALU.mult)
        yield
        self.wout_part(l, 2, yc, "s5")
        A.reset(m0)

    def attn(self, l, dr, seq_start, pos_ap, co=None):
        TT, A = self.TT, self.A
        st = self.st[l]
        m0 = A.mark()
        pc = lambda c, n=1, p0=0, p1=128: self.pvc(l, c, n, p0, p1)
        yb = A.alloc(4 * TT, BF16)
        qT = A.alloc(10 * TT, BF16)
        m1 = A.mark()
        qraw = A.alloc(10 * TT, F32)
        vaug = st["vaug"]
        VA = lambda blk, g, lo=0, hi=65: vaug.v((5, 2, 65), (blk, g, slice(lo, hi)))
        posi = A.alloc(TT, I32)
        kia = A.alloc(TT, I32)
        posf, ang, sn, cs_, tmp, tA, tB = [A.alloc(TT, F32) for _ in range(7)]
        P16 = dict(p0=0, p1=16)
        self.dma("sp", posi.ap[0:16, :], pos_ap.partition_broadcast(16), ("pos", 0), writes=[posi.r(0, TT, 0, 16)])
        self.cp(posf.r(0, TT, **P16), posi.r(0, TT, **P16), eng="dve")
        self.ts(ang.r(0, TT, **P16), posf.r(0, TT, **P16), pc(C_INVF, 1, 0, 16), None, ALU.mult)
        self.sincos(sn.r(0, TT, **P16), cs_.r(0, TT, **P16), ang.r(0, TT, **P16), tmp.r(0, TT, **P16), kia.r(0, TT, **P16))
        self.ts(sn.r(0, TT, **P16), sn.r(0, TT, **P16), pc(C_SGN, 1, 0, 16), None, ALU.mult)
        self.memset(vaug.v((5, 2, 65), (slice(None), slice(None), 64)), 1.0)
        jobs = {17: [(0, 128, lambda b, s: self.cp(vaug.v((5, 2, 65), (1 + s, slice(None), slice(0, 64))),
                                                   b.v((2, 64), (slice(None), slice(None))), eng="act"), True)]}
        hq = lambda t, h, p1=64: t.v((10, TT), (h, slice(None)), 0, p1)
        for h in range(10):
            ch, c0 = (18 + h // 4, (h % 4) * 64) if h < 8 else (20, (h - 8) * 64)
            jobs.setdefault(ch, []).append((c0, 64, (lambda b, h=h: self.cp(hq(qraw, h), b.r(0, TT, 0, 64), eng="act")), False))
        self.proj(l, [17, 18, 19, 20], jobs, "at")
        sq = A.alloc(TT, BF16)
        rs = A.alloc(TT, F32)
        for h in range(10):
            gcol, gscol = (C_QN, C_QNS) if h < 8 else (C_KN, C_KNS)
            self.act(sq.r(0, TT, 0, 64), hq(qraw, h), AF.Square)
            b, _ = self.bank()
            self.mm(b.r(0, TT, 0, 64), self.ones_bf.r(0, 64, 0, 64), sq.r(0, TT, 0, 64))
            self.rsqrt_act(rs.r(0, TT, 0, 64), b.r(0, TT, 0, 64), 1.0 / 64, self.cst.r(0, 1, 0, 64))
            self.stt(hq(qT, h), hq(qraw, h), pc(gcol, 1, 0, 64), rs.r(0, TT, 0, 64), ALU.mult, ALU.mult)
            self.stt(tA.r(0, TT, **P16), hq(qraw, h, 16), pc(gcol, 1, 0, 16), rs.r(0, TT, **P16), ALU.mult, ALU.mult)
            bsw, _ = self.bank()
            self.mm(bsw.r(0, TT, 0, 16), self.perm16.r(0, 16, 0, 16), hq(qraw, h, 16))
            self.stt(tB.r(0, TT, **P16), bsw.r(0, TT, 0, 16), pc(gscol, 1, 0, 16), rs.r(0, TT, **P16), ALU.mult, ALU.mult)
            self.tt(tA.r(0, TT, **P16), tA.r(0, TT, **P16), cs_.r(0, TT, **P16), ALU.mult)
            self.tt(tB.r(0, TT, **P16), tB.r(0, TT, **P16), sn.r(0, TT, **P16), ALU.mult)
            self.tt(hq(qT, h, 16), tA.r(0, TT, **P16), tB.r(0, TT, **P16), ALU.add)
        A.reset(m1)
        E = [A.alloc(512, BF16) for _ in range(2)]
        Pm = [A.alloc(512, BF16) for _ in range(2)]
        ytok = A.alloc(512, F32)
        den = A.alloc(8, F32)
        mcur = bc(self.amask.r(0, 128), (128, 4, 128), 1)
        mprev = bc(self.amask.r(128, 256), (128, 4, 128), 1)
        esk = self.sinks.r(l * 8, (l + 1) * 8)

        def step_co(k):
            if co is not None:
                for _ in range(k):
                    next(co, None)
        for n in range(TT // 128):
            blk = slice(n * 128, (n + 1) * 128)
            has_prev = not (seq_start and n == 0)
            for g in range(2):
                qv = qT.v((10, TT), (slice(4 * g, 4 * g + 4), blk), 0, 64)
                srcs = [(qT.v((10, TT), (8 + g, blk), 0, 64), mcur, 1)]
                if has_prev:
                    kp = (qT.v((10, TT), (8 + g, slice((n - 1) * 128, n * 128)), 0, 64) if n > 0
                          else st["kprev"].v((2, 128), (g, slice(None)), 0, 64))
                    srcs.append((kp, mprev, 0))
                for i, (kT_, msk_, _) in enumerate(srcs):
                    b, _ = self.bank()
                    self.mm(b.v((4, 128), (slice(None), slice(None))), kT_, qv)
                    self.act(E[i].r(), b.r(0, 512), AF.Exp, scale=0.125)
                    self.tt(Pm[i].v((4, 128), (slice(None), slice(None))), E[i].v((4, 128), (slice(None), slice(None))),
                            msk_, ALU.mult, eng="pool")
                bo, _ = self.bank()
                for j in range(4):
                    o_ = bo.v((4, 65), (j, slice(None)))
                    if has_prev:
                        self.mm(o_, Pm[1].r(j * 128, (j + 1) * 128), VA(n, g), start=True, stop=False)
                    self.mm(o_, Pm[0].r(j * 128, (j + 1) * 128), VA(n + 1, g), start=not has_prev, stop=True)
                dn = den.r(4 * g, 4 * g + 4)
                self.tt(dn, bo.v((4, 65), (slice(None), 64)), self.sinks.r(l * 8 + 4 * g, l * 8 + 4 * g + 4), ALU.add)
                self.op("dve", lambda e, dn=dn: e.reciprocal(dn.ap, dn.ap), [dn], [dn])
                self.tt(ytok.v((8, 64), (slice(4 * g, 4 * g + 4), slice(None))), bo.v((4, 65), (slice(None), slice(0, 64))),
                        bc(dn, (128, 4, 64), 2), ALU.mult)
                step_co(5)
            b, _ = self.bank()
            for kt in range(4):
                self.tr(b.r(kt * 128, (kt + 1) * 128), ytok.r(kt * 128, (kt + 1) * 128), self.ident)
            self.cp(yb.v((4, TT), (slice(None), blk)), b.v((4, 128), (slice(None), slice(None))), eng="act")
            step_co(5)
        if co is not None:
            for _ in co:
                pass
        self.cp(st["kprev"].v((2, 128), (slice(None), slice(None)), 0, 64),
                qT.v((10, TT), (slice(8, 10), slice(TT - 128, TT)), 0, 64), eng="pool")
        self.cp(vaug.v((5, 2, 65), (0, slice(None), slice(None))), vaug.v((5, 2, 65), (4, slice(None), slice(None))), eng="pool")
        self.wout_part(l, 1, yb, "at")
        A.reset(m0)

    def reserve(self):
        b, _ = self.bank()
        i = self.banks.index(b)
        self.reserved = getattr(self, "reserved", set())
        self.reserved.add(i)
        return b

    def release(self, b):
        self.reserved.discard(self.banks.index(b))

    def bank(self):
        res = getattr(self, "reserved", set())
        while True:
            i = self.bank_rr % 8
            self.bank_rr += 1
            if i not in res:
                return self.banks[i], self.banks_bf[i]

    def mixer(self, l, dr, seq_start, pos_ap, which=("rwkv", "attn", "s5", "hgrn")):
        self.rmsnorm_fm(l, C_MIX)
        if "rwkv" in which:
            self.rwkv(l, dr, seq_start)
        if "hgrn" in which:
            self.hgrn(l, dr, seq_start)
        if "s5" in which and "attn" in which:
            self.attn(l, dr, seq_start, pos_ap, co=self.s5_gen(l, dr, seq_start))
        else:
            if "s5" in which:
                self.s5(l, dr, seq_start)
            if "attn" in which:
                self.attn(l, dr, seq_start, pos_ap)


def build_program(n_tok, seq_len, nlayers=2, stages=("ffn1", "mix", "ffn2"), which=("rwkv", "attn", "s5", "hgrn")):
    nc = bass.Bass("TRN2", target_bir_lowering=False)
    L = nlayers
    nseq = n_tok // seq_len
    shapes = dict(x=([n_tok, D], F32), pos=([nseq, seq_len], I32), win=([L, D, NCIN], F32), pv=([L, 128, NV], F32),
                  waup=([L, 128, 512], F32), gup=([L, 128, 512], F32), s5b=([L, 128, 4096], F32), s5c=([L, 128, 4096], F32),
                  gluw=([L, 512, 512], F32), wout=([L, D, D], F32), sinks=([L, 128, 8], F32),
                  ffn1_w_gate=([L, D, DFF], F32), ffn1_w_up=([L, D, DFF], F32), ffn1_w_down=([L, DFF, D], F32),
                  ffn2_w_gate=([L, D, DFF], F32), ffn2_w_up=([L, D, DFF], F32), ffn2_w_down=([L, DFF, D], F32))
    if "ffn1" not in stages:
        for k in ("ffn1_w_gate", "ffn1_w_up", "ffn1_w_down"):
            shapes.pop(k)
    if "ffn2" not in stages:
        for k in ("ffn2_w_gate", "ffn2_w_up", "ffn2_w_down"):
            shapes.pop(k)
    dr = {k: nc.dram_tensor(k, s, t, kind="ExternalInput").ap() for k, (s, t) in shapes.items()}
    y = nc.dram_tensor("y", [n_tok, D], F32, kind="ExternalOutput").ap()
    E = Emit(nc, nlayers)
    E.setup_common(dr)
    E.make_scratch(L)
    TT = E.TT
    tps = seq_len // TT
    for it in range(n_tok // TT):
        E.first = (it == 0)
        E.load_x_tile(dr["x"], it)
        sq, ti = it // tps, it % tps
        for l in range(L):
            if "ffn1" in stages:
                E.ffn(l, C_FFN1, "ffn1", "f1")
            if "mix" in stages:
                E.mixer(l, dr, ti == 0, dr["pos"][sq, ti * TT:(ti + 1) * TT], which)
            if "ffn2" in stages:
                E.ffn(l, C_FFN2, "ffn2", "f2")
        E.store_x_tile(y, it)
    E.S.emit()
    return nc, E


_PROG_CACHE = {}


def kernel(**inputs):
    inp = {k: np.asarray(v) for k, v in inputs.items()}
    B, S_, _ = inp["x"].shape
    per = B // NCORES
    packed = pack_all(inp)
    key = (per * S_, S_)
    if key not in _PROG_CACHE:
        _PROG_CACHE[key] = build_program(per * S_, S_)[0]
    nc = _PROG_CACHE[key]
    in_maps = []
    for c in range(NCORES):
        d = dict(packed)
        d["x"] = np.ascontiguousarray(inp["x"][c * per:(c + 1) * per].reshape(per * S_, D))
        d["pos"] = np.ascontiguousarray(inp["positions"][c * per:(c + 1) * per].astype(np.int32))
        in_maps.append(d)
    res = run_bass_kernel_spmd(nc, in_maps, core_ids=list(range(NCORES)))
    out = np.concatenate([np.asarray(r["y"]).reshape(per, S_, D) for r in res.results], axis=0)
    return out.astype(np.float32)
```
